# Optimizing a Trainium2 kernel written in Bass

```python
import jax, jax.numpy as jnp
from jax import lax
import numpy as np

D_MODEL = 2048
BATCH = 2
SEQ = 8192
DEPTH = 4

GRID_W = 64
CTX_LEN = 256
D_CONV = D_MODEL // 2
D_ATTN = D_MODEL - D_CONV
HEAD_DIM = 64
N_HEADS = D_ATTN // HEAD_DIM
N_KV_HEADS = 4
GQA_GROUP = N_HEADS // N_KV_HEADS
WINDOW = 128
BLOCK = 128
CONV_WIDTH = 3
D_FF = 4 * D_MODEL
ROPE_THETA = 10000.0
ROPE_AXIS_DIM = HEAD_DIM // 2
EPS = 1e-6
N_MOD = 6
KV_START = 3 * D_CONV + D_ATTN
D_IN_PROJ = KV_START + 2 * N_KV_HEADS * HEAD_DIM
SCALE = HEAD_DIM ** -0.5
NEG_INF = -1e30

kernel_name = "hybrid_conv_swa_dit_block"


def rmsnorm(x, g):
    xf = x.astype(jnp.float32)
    y = xf * lax.rsqrt(jnp.mean(xf * xf, axis=-1, keepdims=True) + EPS)
    return (y * g.astype(jnp.float32)).astype(x.dtype)


def modulate(h, shift, scale):
    return h * (1 + scale) + shift


def short_conv(u, w, b):
    n = u.shape[1]
    up = jnp.pad(u, ((0, 0), (1, 1), (0, 0)))
    return up[:, 0:n] * w[0] + up[:, 1:n + 1] * w[1] + up[:, 2:n + 2] * w[2] + b


def gated_conv_mixer(p_conv, w, b):
    bg, cg, h = jnp.split(p_conv, 3, axis=-1)
    return bg * short_conv(cg * h, w, b)


def rope_tables(n_tokens, dtype):
    rows = n_tokens // GRID_W
    row_pos = jnp.repeat(jnp.arange(rows, dtype=jnp.float32), GRID_W)
    col_pos = jnp.tile(jnp.arange(GRID_W, dtype=jnp.float32), rows)
    inv = ROPE_THETA ** (-jnp.arange(0, ROPE_AXIS_DIM, 2, dtype=jnp.float32) / ROPE_AXIS_DIM)
    ang_r = row_pos[:, None] * inv[None, :]
    ang_c = col_pos[:, None] * inv[None, :]
    return (jnp.cos(ang_r)[:, None, :].astype(dtype), jnp.sin(ang_r)[:, None, :].astype(dtype),
            jnp.cos(ang_c)[:, None, :].astype(dtype), jnp.sin(ang_c)[:, None, :].astype(dtype))


def rotate(x, cos, sin):
    x1, x2 = jnp.split(x, 2, axis=-1)
    return jnp.concatenate([x1 * cos - x2 * sin, x2 * cos + x1 * sin], axis=-1)


def rope_2d(x, tabs):
    cr, sr, cc, sc = tabs
    xr, xc = jnp.split(x, 2, axis=-1)
    return jnp.concatenate([rotate(xr, cr, sr), rotate(xc, cc, sc)], axis=-1)


def band_mask(nb, n_tokens):
    n = jnp.arange(nb)[:, None, None]
    r = jnp.arange(BLOCK)[None, :, None]
    j = jnp.arange(3 * BLOCK)[None, None, :]
    q_pos = n * BLOCK + r
    k_pos = (n - 1) * BLOCK + j
    return (jnp.abs(k_pos - q_pos) <= WINDOW) & (k_pos >= 0) & (k_pos < n_tokens)


def latent_window_attention(q, k, v, kc, vc, sink):
    bsz, n_tok = q.shape[0], q.shape[1]
    n_ctx = kc.shape[1]
    nb = n_tok // BLOCK
    qb = q.reshape(bsz, nb, BLOCK, N_KV_HEADS, GQA_GROUP, HEAD_DIM)

    def band(t):
        tb = t.reshape(bsz, nb, BLOCK, N_KV_HEADS, HEAD_DIM)
        tb = jnp.pad(tb, ((0, 0), (1, 1), (0, 0), (0, 0), (0, 0)))
        return jnp.concatenate([tb[:, :-2], tb[:, 1:-1], tb[:, 2:]], axis=2)

    kw, vw = band(k), band(v)
    s_loc = jnp.einsum('bnqhgd,bnkhd->bnhgqk', qb, kw).astype(jnp.float32) * SCALE
    s_loc = jnp.where(band_mask(nb, n_tok)[None, :, None, None], s_loc, NEG_INF)
    s_ctx = jnp.einsum('bnqhgd,bchd->bnhgqc', qb, kc).astype(jnp.float32) * SCALE
    snk = jnp.broadcast_to(sink.astype(jnp.float32).reshape(1, 1, N_KV_HEADS, GQA_GROUP, 1, 1),
                           s_loc.shape[:-1] + (1,))
    p = jax.nn.softmax(jnp.concatenate([s_loc, s_ctx, snk], axis=-1), axis=-1).astype(v.dtype)
    nk = 3 * BLOCK
    o = (jnp.einsum('bnhgqk,bnkhd->bnqhgd', p[..., :nk], vw)
         + jnp.einsum('bnhgqc,bchd->bnqhgd', p[..., nk:nk + n_ctx], vc))
    return o.reshape(bsz, n_tok, D_ATTN)


def context_attention(qc, kc, vc, sink):
    bsz, n_ctx = qc.shape[0], qc.shape[1]
    qg = qc.reshape(bsz, n_ctx, N_KV_HEADS, GQA_GROUP, HEAD_DIM)
    s = jnp.einsum('blhgd,bchd->bhglc', qg, kc).astype(jnp.float32) * SCALE
    snk = jnp.broadcast_to(sink.astype(jnp.float32).reshape(1, N_KV_HEADS, GQA_GROUP, 1, 1),
                           s.shape[:-1] + (1,))
    p = jax.nn.softmax(jnp.concatenate([s, snk], axis=-1), axis=-1).astype(vc.dtype)
    o = jnp.einsum('bhglc,bchd->blhgd', p[..., :n_ctx], vc)
    return o.reshape(bsz, n_ctx, D_ATTN)


def mixer_merge(conv_out, attn_out, g_oc, g_oa, w_out):
    return jnp.concatenate([rmsnorm(conv_out, g_oc), rmsnorm(attn_out, g_oa)], axis=-1) @ w_out


def sq_relu_mlp(h, w1, w2):
    return jnp.square(jax.nn.relu(h @ w1)) @ w2


def setup_inputs(seed: int = 0) -> dict:
    key = jax.random.key(seed)
    ks = jax.random.split(key, 18)

    def nrm(k, shape, s):
        return jax.random.normal(k, shape, jnp.float32) * s

    return {
        "x": nrm(ks[0], (BATCH, SEQ, D_MODEL), 1.0),
        "c": nrm(ks[1], (BATCH, D_MODEL), 1.0),
        "ctx": nrm(ks[2], (BATCH, CTX_LEN, D_MODEL), 1.0),
        "c_ctx": nrm(ks[3], (D_MODEL,), 1.0),
        "w_ada": nrm(ks[4], (DEPTH, D_MODEL, N_MOD * D_MODEL), 0.5 * D_MODEL ** -0.5),
        "b_ada": nrm(ks[5], (DEPTH, N_MOD * D_MODEL), 0.02),
        "g_norm1": 1.0 + nrm(ks[6], (DEPTH, D_MODEL), 0.02),
        "g_norm2": 1.0 + nrm(ks[7], (DEPTH, D_MODEL), 0.02),
        "w_in": nrm(ks[8], (DEPTH, D_MODEL, D_IN_PROJ), D_MODEL ** -0.5),
        "conv_w": nrm(ks[9], (DEPTH, CONV_WIDTH, D_CONV), CONV_WIDTH ** -0.5),
        "conv_b": nrm(ks[10], (DEPTH, D_CONV), 0.02),
        "sink": nrm(ks[11], (DEPTH, N_HEADS), 0.5),
        "g_out_conv": 1.0 + nrm(ks[12], (DEPTH, D_CONV), 0.02),
        "g_out_attn": 1.0 + nrm(ks[13], (DEPTH, D_ATTN), 0.02),
        "w_out": nrm(ks[14], (DEPTH, D_MODEL, D_MODEL), D_MODEL ** -0.5),
        "w_mlp1": nrm(ks[15], (DEPTH, D_MODEL, D_FF), D_MODEL ** -0.5),
        "w_mlp2": nrm(ks[16], (DEPTH, D_FF, D_MODEL), D_FF ** -0.5),
        "g_final": 1.0 + nrm(ks[17], (D_MODEL,), 0.02),
    }


def reference(x, c, ctx, c_ctx, w_ada, b_ada, g_norm1, g_norm2, w_in, conv_w, conv_b, sink,
              g_out_conv, g_out_attn, w_out, w_mlp1, w_mlp2, g_final):
    bsz, n_tok, _ = x.shape
    n_ctx = ctx.shape[1]
    tabs = rope_tables(n_tok, x.dtype)
    sc = jax.nn.silu(c)
    scc = jax.nn.silu(c_ctx)
    for i in range(DEPTH):
        last = i == DEPTH - 1
        m = jnp.split((sc @ w_ada[i] + b_ada[i])[:, None, :], N_MOD, axis=-1)
        mc = jnp.split(scc @ w_ada[i] + b_ada[i], N_MOD, axis=-1)

        h = modulate(rmsnorm(x, g_norm1[i]), m[0], m[1])
        hc = modulate(rmsnorm(ctx, g_norm1[i]), mc[0], mc[1])
        p = h @ w_in[i]
        p_conv = p[..., :3 * D_CONV]
        q = p[..., 3 * D_CONV:KV_START].reshape(bsz, n_tok, N_HEADS, HEAD_DIM)
        k, v = jnp.split(p[..., KV_START:], 2, axis=-1)
        q = rope_2d(q, tabs)
        k = rope_2d(k.reshape(bsz, n_tok, N_KV_HEADS, HEAD_DIM), tabs)
        v = v.reshape(bsz, n_tok, N_KV_HEADS, HEAD_DIM)
        kc, vc = jnp.split(hc @ w_in[i][:, KV_START:], 2, axis=-1)
        kc = kc.reshape(bsz, n_ctx, N_KV_HEADS, HEAD_DIM)
        vc = vc.reshape(bsz, n_ctx, N_KV_HEADS, HEAD_DIM)

        conv_out = gated_conv_mixer(p_conv, conv_w[i], conv_b[i])
        attn_out = latent_window_attention(q, k, v, kc, vc, sink[i])
        x = x + m[2] * mixer_merge(conv_out, attn_out, g_out_conv[i], g_out_attn[i], w_out[i])

        if not last:
            pc = hc @ w_in[i][:, :KV_START]
            ctx_conv = gated_conv_mixer(pc[..., :3 * D_CONV], conv_w[i], conv_b[i])
            qc = pc[..., 3 * D_CONV:].reshape(bsz, n_ctx, N_HEADS, HEAD_DIM)
            ctx_attn = context_attention(qc, kc, vc, sink[i])
            ctx = ctx + mc[2] * mixer_merge(ctx_conv, ctx_attn, g_out_conv[i], g_out_attn[i], w_out[i])

        x = x + m[5] * sq_relu_mlp(modulate(rmsnorm(x, g_norm2[i]), m[3], m[4]), w_mlp1[i], w_mlp2[i])
        if not last:
            ctx = ctx + mc[5] * sq_relu_mlp(modulate(rmsnorm(ctx, g_norm2[i]), mc[3], mc[4]),
                                            w_mlp1[i], w_mlp2[i])
    return rmsnorm(x, g_final)
```

```python
from contextlib import ExitStack

import numpy as np

import concourse.bass as bass
import concourse.mybir as mybir
from concourse.bass_utils import run_bass_kernel_spmd

F32 = mybir.dt.float32
BF16 = mybir.dt.bfloat16
AF = mybir.ActivationFunctionType
ALU = mybir.AluOpType

PE, ACT, DVE, POOL, SP = 0, 1, 2, 3, 4

D = 2048
KC = 16
DEPTH = 4
SEQ = 8192
NCTX = 256
EXT = 3072
NBE = 24
OWNB = 4
TMAX = 384
WMAX = 514
DFF = 8192
EPS = 1e-6
NEG = -30000.0
RK = 6
NSLAB = 45


class Res:
    __slots__ = ("w", "r")

    def __init__(self):
        self.w = None
        self.r = {}


class Sched:
    def __init__(self, nc, stack):
        self.nc = nc
        self.stack = stack
        self.h = [nc.tensor, nc.scalar, nc.vector, nc.gpsimd, nc.sync]
        self.sem = []
        self.cnt = []
        for n in ["pe", "act", "dve", "pool", "sp"]:
            self.sem.append(stack.enter_context(nc.semaphore("s_" + n)))
            self.cnt.append(0)
        self.known = [dict() for _ in range(5)]

    def new_dma_counter(self, name):
        self.sem.append(self.stack.enter_context(self.nc.semaphore(name)))
        self.cnt.append(0)
        return len(self.sem) - 1

    def _deps(self, reads, writes):
        deps = {}
        for r in reads:
            if r.w is not None:
                c, v = r.w
                if deps.get(c, 0) < v:
                    deps[c] = v
        for w in writes:
            if w.w is not None:
                c, v = w.w
                if deps.get(c, 0) < v:
                    deps[c] = v
            for c, v in w.r.items():
                if deps.get(c, 0) < v:
                    deps[c] = v
        return deps

    def _wait(self, eng, deps):
        kn = self.known[eng]
        for c, v in deps.items():
            if c == eng and eng == PE:
                continue
            if kn.get(c, 0) >= v:
                continue
            self.h[eng].wait_ge(self.sem[c], v)
            kn[c] = v

    def _mark(self, reads, writes, ticket):
        c, v = ticket
        for r in reads:
            r.r[c] = v
        for w in writes:
            w.w = ticket
            w.r = {}

    def op(self, eng, reads, writes, fn):
        self._wait(eng, self._deps(reads, writes))
        ins = fn()
        self.cnt[eng] += 1
        ins.then_inc(self.sem[eng], 1)
        self._mark(reads, writes, (eng, self.cnt[eng]))

    def group(self, eng, reads, writes, fns):
        self._wait(eng, self._deps(reads, writes))
        ins = None
        for fn in fns:
            ins = fn()
        self.cnt[eng] += 1
        ins.then_inc(self.sem[eng], 1)
        self._mark(reads, writes, (eng, self.cnt[eng]))

    def dma(self, q, dc, reads, writes, fn):
        deps = self._deps(reads, writes)
        if self.cnt[dc] > 0:
            deps[dc] = max(deps.get(dc, 0), self.cnt[dc])
        self._wait(q, deps)
        ins = fn()
        self.cnt[dc] += 16
        ins.then_inc(self.sem[dc], 16)
        self._mark(reads, writes, (dc, self.cnt[dc]))


class Pool:
    def __init__(self, aps):
        self.aps = aps
        self.res = [Res() for _ in aps]
        self.i = 0

    def get(self):
        i = self.i
        self.i = (i + 1) % len(self.aps)
        return self.aps[i], self.res[i]


def layer_tiles(l):
    lo, hi = 1 + l, 23 - l
    sizes = [2]
    rem = hi - lo - 2
    while rem > 0:
        nb = min(3, rem)
        sizes.append(nb)
        rem -= nb
    if sizes[-1] == 1:
        sizes[-2:] = [2, 2]
    tiles = []
    b = lo
    for i, nb in enumerate(sizes):
        tiles.append((b, nb, i == 0))
        b += nb
    assert b == hi
    return tiles


def slab_list():
    sl = []
    for c0 in (4096, 3072, 3584, 1024, 2048, 0, 1536, 2560, 512):
        sl.append(("in", 0, c0))
    for og in range(4):
        sl.append(("out", 0, og * 512))
    for j in range(4):
        for s in range(4):
            sl.append(("m1", 0, (4 * j + s) * 512))
        for g in range(4):
            sl.append(("m2", j * 2048, g * 512))
    return sl


def build_nc(depth=DEPTH):
    nc = bass.Bass("TRN2", target_bir_lowering=False)

    def din(name, shape, dt=F32):
        return nc.dram_tensor(name, shape, dt, kind="ExternalInput").ap()

    def dint(name, shape, dt):
        return nc.dram_tensor(name, shape, dt, kind="Internal").ap()

    x0 = din("x0", [D, EXT])
    ctx0 = din("ctx0", [D, NCTX])
    cvec = din("cvec", [128, KC, 2])
    w_ada = din("w_ada", [DEPTH, D, 6 * D])
    b_ada = din("b_ada", [128, DEPTH, 96])
    gvec = din("gvec", [128, DEPTH, 80])
    gfin = din("gfin", [128, KC])
    sinkr = din("sinkr", [128, DEPTH * 16])
    Wd = {"in": din("w_in", [DEPTH, D, 4608]), "out": din("w_out", [DEPTH, D, D]),
          "m1": din("w_mlp1", [DEPTH, D, DFF]), "m2": din("w_mlp2", [DEPTH, DFF, D])}
    ropeC = din("ropeC", [128, EXT])
    ropeS = din("ropeS", [128, EXT])
    kbias_d = din("kbias", [128, NBE])
    emask_d = din("emask", [128, 2])
    cmat = din("cmat", [128, 4, 128])
    y = nc.dram_tensor("y", [D, 2048], F32, kind="ExternalOutput").ap()

    xs = [dint("xsA", [D, EXT], F32), dint("xsB", [D, EXT], F32)]
    cs = [dint("csA", [D, NCTX], F32), dint("csB", [D, NCTX], F32)]
    Wb = {"in": dint("wb_in", [DEPTH, D, 4608], BF16), "out": dint("wb_out", [DEPTH, D, D], BF16),
          "m1": dint("wb_m1", [DEPTH, D, DFF], BF16), "m2": dint("wb_m2", [DEPTH, DFF, D], BF16)}

    def fm(ap2d):
        return ap2d.rearrange("(c p) n -> p c n", p=128)

    with ExitStack() as st:
        S = Sched(nc, st)

        def sb(name, shape, dt):
            return st.enter_context(nc.sbuf_tensor(name, shape, dt))

        xt = sb("xt", [128, KC, WMAX], F32)
        hb = sb("hb", [128, KC, WMAX], BF16)
        slabs = sb("slabs", [128, 3, KC, 512], BF16)
        hid = sb("hid", [128, KC, TMAX], BF16)
        hh2 = sb("hh2", [128, KC, 2], BF16)
        qrot = sb("qrot", [128, 8, TMAX], BF16)
        Kpad = sb("Kpad", [128, 4, 2, RK * 128], BF16)
        Vd = sb("Vd", [128, RK, 4, 2, 64], BF16)
        Kcp = sb("Kcp", [128, 4, 2, NCTX], BF16)
        Vcd = sb("Vcd", [128, 2, 4, 2, 64], BF16)
        PT = sb("PT", [128, 2, 5, 512], BF16)
        ys = sb("ys", [128, 4, TMAX], F32)
        cog = sb("cog", [128, 8, TMAX], BF16)
        aog = sb("aog", [128, 8, TMAX], BF16)
        rstd1 = sb("rstd1", [128, WMAX], F32)
        rstd2 = sb("rstd2", [128, TMAX], F32)
        rstc = sb("rstc", [128, TMAX], F32)
        rsta = sb("rsta", [128, TMAX], F32)
        rpC = sb("rpC", [128, WMAX], F32)
        rpS = sb("rpS", [128, WMAX], F32)
        wf_t = sb("wf", [128, 6, WMAX], F32)
        wb_t = sb("wbp", [128, 4, WMAX], BF16)
        cm_f = sb("cm_f", [128, 4, 128], F32)
        ident_b = sb("ident_b", [128, 128], BF16)
        ones_b = sb("ones_b", [128, 128], BF16)
        maskL4 = sb("maskL4", [128, 512], BF16)
        maskR4 = sb("maskR4", [128, 512], BF16)
        kbias = sb("kbias_s", [128, NBE], F32)
        emask = sb("emask_s", [128, 2], F32)
        es = sb("es", [128, DEPTH * 16], F32)
        mod = sb("mod", [128, DEPTH, 96, 2], F32)
        bada = sb("bada", [128, DEPTH, 96], F32)
        gv = sb("gv", [128, DEPTH, 80], F32)
        gf = sb("gf", [128, KC], F32)
        cv = sb("cv", [128, KC, 2], F32)
        sv = sb("sv", [128, KC, 2], F32)
        AA = sb("AA", [128, DEPTH, 2, KC, 2], F32)
        ps = st.enter_context(nc.psum_tensor("ps", [128, 8, 512], F32))

        permR = cm_f[:, 1, :]
        R_bank = [Res() for _ in range(8)]
        gen_i = [0]

        gen_n = [4]

        def bank():
            i = gen_i[0] % gen_n[0]
            gen_i[0] = (i + 1) % gen_n[0]
            return ps[:, i, :], R_bank[i]

        wf = Pool([wf_t[:, i, :] for i in range(6)])
        wbp = Pool([wb_t[:, i, :] for i in range(4)])

        R_xt = [Res() for _ in range(KC)]
        R_hb = [Res() for _ in range(KC)]
        R_slab = [Res() for _ in range(3)]
        R_hid = [Res() for _ in range(KC)]
        R_hh2 = Res()
        R_qrot = [Res() for _ in range(8)]
        R_K = [Res() for _ in range(RK)]
        R_V = [Res() for _ in range(RK)]
        R_Kc = [Res(), Res()]
        R_Vc = [Res(), Res()]
        R_PT = [[Res() for _ in range(5)] for _ in range(2)]
        R_ys = [Res() for _ in range(4)]
        R_cog = [Res() for _ in range(8)]
        R_aog = [Res() for _ in range(8)]
        R_r1, R_r2, R_rc, R_ra = Res(), Res(), Res(), Res()
        R_rope = Res()
        R_const = Res()
        R_mod = [Res() for _ in range(DEPTH)]
        R_AA = [Res() for _ in range(DEPTH)]
        R_sv = Res()
        R_xs = [[[Res() for _ in range(KC)] for _ in range(NBE)] for _ in range(2)]
        R_cs = [[Res() for _ in range(KC)] for _ in range(2)]
        R_prog = [[Res() for _ in range(12)] for _ in range(DEPTH)]
        R_wb = [[Res() for _ in range(NSLAB)] for _ in range(DEPTH)]

        d_misc = S.new_dma_counter("d_misc")
        d_x = [S.new_dma_counter("d_x%d" % i) for i in range(4)]
        d_rope = [S.new_dma_counter("d_rp%d" % i) for i in range(2)]
        d_st = [S.new_dma_counter("d_st%d" % i) for i in range(4)]
        d_slab = [S.new_dma_counter("d_sl%d" % i) for i in range(3)]
        d_cast = [S.new_dma_counter("d_c%d" % i) for i in range(NSLAB)]
        st_i = [0]
        slab_i = [0]

        SL = slab_list()

        def emit_casts(l, i0, i1, gate):
            for idx in range(i0, i1):
                nm, r0, c0 = SL[idx]
                src = Wd[nm][l, r0:r0 + 2048, c0:c0 + 512]
                dst = Wb[nm][l, r0:r0 + 2048, c0:c0 + 512]
                S.dma(POOL, d_cast[idx], gate, [R_wb[l][idx]],
                      lambda src=src, dst=dst: nc.gpsimd.dma_start(out=dst, in_=src))

        emit_casts(0, 0, NSLAB, [])

        def load_slab(l, idx):
            nm, r0, c0 = SL[idx]
            slot = slab_i[0]
            slab_i[0] = (slot + 1) % 3
            src = fm(Wb[nm][l, r0:r0 + 2048, c0:c0 + 512])
            S.dma(SP, d_slab[slot], [R_wb[l][idx]], [R_slab[slot]],
                  lambda: nc.sync.dma_start(out=slabs[:, slot, :, :], in_=src))
            return slabs[:, slot, :, :], R_slab[slot]

        def ld(dst, src, res):
            S.dma(ACT, d_misc, [], [res], lambda: nc.scalar.dma_start(out=dst, in_=src))

        ld(cm_f[:], cmat[:, :, :], R_const)
        ld(kbias[:], kbias_d[:, :], R_const)
        ld(emask[:], emask_d[:, :], R_const)
        ld(es[:], sinkr[:, :], R_const)
        ld(bada[:], b_ada[:, :, :], R_const)
        ld(gv[:], gvec[:, :, :], R_const)
        ld(gf[:], gfin[:, :], R_const)
        ld(cv[:], cvec[:, :, :], R_const)
        C = [R_const]
        S.op(DVE, C, C, lambda: nc.vector.tensor_copy(out=ident_b[:], in_=cm_f[:, 0, :]))
        S.op(DVE, C, C, lambda: nc.vector.memset(ones_b[:], 1.0))
        for g4 in range(4):
            S.op(DVE, C, C, lambda g4=g4: nc.vector.tensor_copy(out=maskL4[:, g4 * 128:(g4 + 1) * 128], in_=cm_f[:, 2, :]))
            S.op(DVE, C, C, lambda g4=g4: nc.vector.tensor_copy(out=maskR4[:, g4 * 128:(g4 + 1) * 128], in_=cm_f[:, 3, :]))
        S.op(DVE, [], R_K, lambda: nc.vector.memset(Kpad[:], 0.0))
        S.op(DVE, [], R_Kc, lambda: nc.vector.memset(Kcp[:], 0.0))
        S.op(ACT, C, C, lambda: nc.scalar.activation(out=es[:], in_=es[:], func=AF.Exp))
        S.op(ACT, C, [R_sv], lambda: nc.scalar.activation(out=sv[:], in_=cv[:], func=AF.Silu))

        def emit_adaln(l, s0, s1):
            for sl in range(s0, s1):
                slot = slab_i[0]
                slab_i[0] = (slot + 1) % 3
                view = slabs[:, slot, :, :].bitcast(F32)
                src = fm(w_ada[l, :, sl * 256:(sl + 1) * 256])
                S.dma(SP, d_slab[slot], [], [R_slab[slot]],
                      lambda view=view, src=src: nc.sync.dma_start(out=view, in_=src))
                for c2 in range(2):
                    cc = sl * 2 + c2
                    bk, rb = bank()
                    fns = [(lambda kc=kc: nc.tensor.matmul(bk[:, 0:2], lhsT=view[:, kc, c2 * 128:(c2 + 1) * 128],
                                                           rhs=sv[:, kc, :], start=(kc == 0), stop=(kc == KC - 1)))
                           for kc in range(KC)]
                    S.group(PE, [R_slab[slot], R_sv], [rb], fns)
                    S.op(ACT, [rb, R_const], [R_mod[l]],
                         lambda cc=cc, bk=bk: nc.scalar.activation(out=mod[:, l, cc, :], in_=bk[:, 0:2], func=AF.Identity,
                                                                  bias=bada[:, l, cc:cc + 1]))

        ada_pending = []

        def adaln_hook():
            if ada_pending:
                la_, sl_ = ada_pending.pop(0)
                emit_adaln(la_, sl_, sl_ + 1)

        def emit_AA(l):
            for ni, (mi, g0) in enumerate(((1, 0), (4, 16))):
                for w in range(2):
                    S.op(DVE, [R_mod[l], R_const], [R_AA[l]],
                         lambda ni=ni, mi=mi, g0=g0, w=w: nc.vector.scalar_tensor_tensor(
                             out=AA[:, l, ni, :, w], in0=mod[:, l, mi * 16:(mi + 1) * 16, w], scalar=1.0,
                             in1=gv[:, l, g0:g0 + 16], op0=ALU.add, op1=ALU.mult))

        def rstd_from(bk, rb, out_ap, rres, n, dim):
            S.op(DVE, [rb], [rres], lambda: nc.vector.tensor_scalar(out=out_ap, in0=bk[:, 0:n], scalar1=1.0 / dim, scalar2=EPS,
                                                                    op0=ALU.mult, op1=ALU.add))
            S.op(ACT, [rres], [rres], lambda: nc.scalar.activation(out=out_ap, in_=out_ap, func=AF.Ln))
            S.op(ACT, [rres], [rres], lambda: nc.scalar.activation(out=out_ap, in_=out_ap, func=AF.Exp, scale=-0.5))

        def mm_group(bk_ap, rb, slab, rs, col0, rhs_fn, rhs_res, n, split=False):
            fns = [(lambda kc=kc: nc.tensor.matmul(bk_ap, lhsT=slab[:, kc, col0:col0 + 128], rhs=rhs_fn(kc),
                                                   start=(kc == 0), stop=(kc == KC - 1))) for kc in range(KC)]
            if split and len(rhs_res) == KC:
                for k0 in range(0, KC, 4):
                    S.group(PE, [rs] + rhs_res[k0:k0 + 4], [rb], fns[k0:k0 + 4])
            else:
                S.group(PE, [rs] + rhs_res, [rb], fns)

        pe_defer = []
        act_defer = []

        def flush_act():
            while act_defer:
                act_defer.pop(0)()

        def flush_defer(keep=0):
            while len(pe_defer) > keep:
                pe_defer.pop(0)()

        off = DEPTH - depth

        def emit_tile(l, kind, B0, nb, first):
            last = (l == depth - 1)
            is_ctx = (kind == "ctx")
            w = 1 if is_ctx else 0
            T = nb * 128
            t0 = B0 * 128
            if is_ctx:
                ts, m0, W, la = 0, 0, NCTX, 0
                kvblocks = [0, 1]
                kv0 = 0
            elif first:
                ts, m0, W, la = t0 - 128, 128, T + 256, 128
                kvblocks = list(range(B0 - 1, B0 + nb + 1))
                kv0 = 0
            else:
                ts, m0, W, la = t0 - 1, 1, T + 129, 128
                kvblocks = list(range(B0 + 1, B0 + nb + 1))
                kv0 = m0 + 128
            kvW = len(kvblocks) * 128
            kv_only = is_ctx and last
            if is_ctx:
                src = ctx0 if l == 0 else cs[(l - 1) % 2]
                src_res = (lambda kc: []) if l == 0 else (lambda kc: [R_cs[(l - 1) % 2][kc]])
            else:
                src = x0 if l == 0 else xs[(l - 1) % 2]
                src_res = (lambda kc: []) if l == 0 else (lambda kc: [R_xs[(l - 1) % 2][b][kc] for b in range(ts // 128, (ts + W - 1) // 128 + 1)])
            srcv = fm(src)
            A1 = lambda kc: AA[:, l, 0, kc, w:w + 1]
            B1 = lambda kc: mod[:, l, kc, w:w + 1]
            G1 = lambda kc: mod[:, l, 32 + kc, w:w + 1]
            B2 = lambda kc: mod[:, l, 48 + kc, w:w + 1]
            A2 = lambda kc: AA[:, l, 1, kc, w:w + 1]
            G2 = lambda kc: mod[:, l, 80 + kc, w:w + 1]
            MR = [R_mod[l], R_AA[l], R_const]

            for q4 in range(4):
                S.dma(ACT, d_x[q4], sum([src_res(kc) for kc in range(q4 * 4, q4 * 4 + 4)], []), R_xt[q4 * 4:(q4 + 1) * 4],
                      lambda q4=q4: nc.scalar.dma_start(out=xt[:, q4 * 4:(q4 + 1) * 4, 0:W], in_=srcv[:, q4 * 4:(q4 + 1) * 4, ts:ts + W]))
            if not is_ctx:
                S.dma(ACT, d_rope[0], [], [R_rope], lambda: nc.scalar.dma_start(out=rpC[:, 0:W], in_=ropeC[:, ts:ts + W]))
                S.dma(ACT, d_rope[1], [], [R_rope], lambda: nc.scalar.dma_start(out=rpS[:, 0:W], in_=ropeS[:, ts:ts + W]))

            WA = W - la
            bkA, rA = ps[:, 6, :], R_bank[6]
            bkB, rB = ps[:, 7, :], R_bank[7]
            for kc in range(KC):
                sq, rsq = wbp.get()
                S.op(ACT, [R_xt[kc]], [rsq], lambda kc=kc, sq=sq: nc.scalar.activation(out=sq[:, 0:W], in_=xt[:, kc, 0:W], func=AF.Square))
                fns = [lambda kc=kc, sq=sq: nc.tensor.matmul(bkA[:, 0:WA], lhsT=ones_b[:], rhs=sq[:, 0:WA], start=(kc == 0), stop=(kc == KC - 1))]
                wr = [rA]
                if la:
                    fns.append(lambda kc=kc, sq=sq: nc.tensor.matmul(bkB[:, 0:la], lhsT=ones_b[:], rhs=sq[:, WA:W], start=(kc == 0), stop=(kc == KC - 1)))
                    wr = [rA, rB]
                S.group(PE, [rsq, R_const], wr, fns)
            rstd_from(bkA, rA, rstd1[:, 0:WA], R_r1, WA, D)
            if la:
                rstd_from(bkB, rB, rstd1[:, WA:W], R_r1, la, D)
            for kc in range(KC):
                t, rt = wf.get()
                S.op(DVE, [R_xt[kc], R_r1], [rt], lambda kc=kc, t=t: nc.vector.tensor_tensor(out=t[:, 0:W], in0=xt[:, kc, 0:W], in1=rstd1[:, 0:W], op=ALU.mult))
                S.op(ACT, [rt] + MR, [R_hb[kc]], lambda kc=kc, t=t: nc.scalar.activation(out=hb[:, kc, 0:W], in_=t[:, 0:W], func=AF.Identity,
                                                                                          scale=A1(kc), bias=B1(kc)))

            slab, rs = load_slab(l, 0)
            for kk in range(2):
                bk, rb = bank()
                mm_group(bk[:, 0:kvW], rb, slab, rs, kk * 128, lambda kc: hb[:, kc, kv0:kv0 + kvW], R_hb, kvW, split=(kk == 0))
                if is_ctx:
                    srcs = lambda rows, c: bk[rows, c:c + 128]
                    rsrc = [rb]
                    for bi, blk in enumerate(kvblocks):
                        for half in range(2):
                            g = 2 * kk + half
                            sr = slice(half * 64, half * 64 + 64)
                            dr = slice((1 - half) * 64, (1 - half) * 64 + 64)
                            S.op(DVE, rsrc, [R_Kc[blk]], lambda g=g, half=half, sr=sr, bi=bi, blk=blk: nc.vector.tensor_copy(
                                out=Kcp[sr, g, half, blk * 128:(blk + 1) * 128], in_=bk[sr, bi * 128:(bi + 1) * 128]))
                            S.op(DVE, rsrc, [R_Kc[blk]], lambda g=g, half=half, sr=sr, dr=dr, bi=bi, blk=blk: nc.vector.tensor_copy(
                                out=Kcp[dr, g, 1 - half, blk * 128:(blk + 1) * 128], in_=bk[sr, bi * 128:(bi + 1) * 128]))
                else:
                    kf, rkf = wf.get()
                    S.op(ACT, [rb], [rkf], lambda kf=kf, bk=bk: nc.scalar.copy(out=kf[:, 0:kvW], in_=bk[:, 0:kvW]))
                    bk2, rb2 = bank()
                    S.group(PE, [rkf, R_const], [rb2], [lambda kf=kf, bk2=bk2: nc.tensor.matmul(bk2[:, 0:kvW], lhsT=permR, rhs=kf[:, 0:kvW], start=True, stop=True)])
                    t1, rt1 = wf.get()
                    S.op(DVE, [rb2, R_rope], [rt1], lambda t1=t1, bk2=bk2: nc.vector.tensor_tensor(out=t1[:, 0:kvW], in0=bk2[:, 0:kvW], in1=rpS[:, kv0:kv0 + kvW], op=ALU.mult))
                    S.op(DVE, [rkf, R_rope], [rkf], lambda kf=kf: nc.vector.tensor_tensor(out=kf[:, 0:kvW], in0=kf[:, 0:kvW], in1=rpC[:, kv0:kv0 + kvW], op=ALU.mult))
                    for bi, blk in enumerate(kvblocks):
                        slot = blk % RK
                        for half in range(2):
                            g = 2 * kk + half
                            sr = slice(half * 64, half * 64 + 64)
                            dr = slice((1 - half) * 64, (1 - half) * 64 + 64)
                            S.op(DVE, [rt1, rkf], [R_K[slot]], lambda g=g, half=half, sr=sr, bi=bi, slot=slot, t1=t1, kf=kf: nc.vector.tensor_tensor(
                                out=Kpad[sr, g, half, slot * 128:(slot + 1) * 128], in0=t1[sr, bi * 128:(bi + 1) * 128],
                                in1=kf[sr, bi * 128:(bi + 1) * 128], op=ALU.add))
                            S.op(DVE, [rt1, rkf], [R_K[slot]], lambda g=g, half=half, sr=sr, dr=dr, bi=bi, slot=slot, t1=t1, kf=kf: nc.vector.tensor_tensor(
                                out=Kpad[dr, g, 1 - half, slot * 128:(slot + 1) * 128], in0=t1[sr, bi * 128:(bi + 1) * 128],
                                in1=kf[sr, bi * 128:(bi + 1) * 128], op=ALU.add))
            for bi, blk in enumerate(kvblocks):
                c = kv0 + bi * 128
                bk, rb = bank()
                fns = [(lambda kc=kc, c=c, bk=bk: nc.tensor.matmul(bk[:, 0:256], lhsT=hb[:, kc, c:c + 128], rhs=slab[:, kc, 256:512],
                                                                  start=(kc == 0), stop=(kc == KC - 1))) for kc in range(KC)]
                S.group(PE, [rs] + R_hb, [rb], fns)
                if is_ctx:
                    vdst, rv = Vcd[:, blk], R_Vc[blk]
                else:
                    vdst, rv = Vd[:, blk % RK], R_V[blk % RK]
                bsrc = bk[:, 0:256].rearrange("p (g d) -> p g d", g=4)
                S.op(ACT, [rb], [rv], lambda vdst=vdst, bsrc=bsrc: nc.scalar.copy(out=vdst[:, :, 0, :], in_=bsrc))
                S.op(DVE, [rb], [rv], lambda vdst=vdst, bsrc=bsrc: nc.vector.tensor_copy(out=vdst[:, :, 1, :], in_=bsrc))
            if kv_only:
                return

            for qs in range(2):
                slab, rs = load_slab(l, 1 + qs)
                for c4 in range(4):
                    ch = qs * 4 + c4
                    bk, rb = bank()
                    mm_group(bk[:, 0:T], rb, slab, rs, c4 * 128, lambda kc: hb[:, kc, m0:m0 + T], R_hb, T)
                    if is_ctx:
                        S.op(ACT, [rb], [R_qrot[ch]], lambda ch=ch, bk=bk: nc.scalar.copy(out=qrot[:, ch, 0:T], in_=bk[:, 0:T]))
                    else:
                        qf, rqf = wf.get()
                        S.op(ACT, [rb], [rqf], lambda qf=qf, bk=bk: nc.scalar.copy(out=qf[:, 0:T], in_=bk[:, 0:T]))
                        bk2, rb2 = bank()
                        S.group(PE, [rqf, R_const], [rb2], [lambda qf=qf, bk2=bk2: nc.tensor.matmul(bk2[:, 0:T], lhsT=permR, rhs=qf[:, 0:T], start=True, stop=True)])
                        t1, rt1 = wf.get()
                        S.op(DVE, [rb2, R_rope], [rt1], lambda t1=t1, bk2=bk2: nc.vector.tensor_tensor(out=t1[:, 0:T], in0=bk2[:, 0:T], in1=rpS[:, m0:m0 + T], op=ALU.mult))
                        S.op(DVE, [rqf, R_rope], [rqf], lambda qf=qf: nc.vector.tensor_tensor(out=qf[:, 0:T], in0=qf[:, 0:T], in1=rpC[:, m0:m0 + T], op=ALU.mult))
                        S.op(DVE, [rt1, rqf], [R_qrot[ch]], lambda ch=ch, t1=t1, qf=qf: nc.vector.tensor_tensor(out=qrot[:, ch, 0:T], in0=t1[:, 0:T], in1=qf[:, 0:T], op=ALU.add))

            bkC, rC = ps[:, 6, :], R_bank[6]
            NH = T + 2
            for half in range(2):
                cgS, rcg = load_slab(l, 3 + half * 3)
                hhS, rhh = load_slab(l, 4 + half * 3)
                for c4 in range(4):
                    ch = half * 4 + c4
                    u_list = []
                    for (sl_, rsl) in ((cgS, rcg), (hhS, rhh)):
                        bk, rb = bank()
                        if is_ctx:
                            mm_group(bk[:, 1:T + 1], rb, sl_, rsl, c4 * 128, lambda kc: hb[:, kc, 0:T], R_hb, T)
                        else:
                            mm_group(bk[:, 0:NH], rb, sl_, rsl, c4 * 128, lambda kc: hb[:, kc, m0 - 1:m0 + T + 1], R_hb, NH)
                        u_list.append((bk, rb))
                    (bcg, rbcg), (bhh, rbhh) = u_list
                    cgs, rcgs = wf.get()
                    u, ru = wf.get()
                    if is_ctx:
                        S.op(ACT, [rbcg], [rcgs], lambda cgs=cgs, bcg=bcg: nc.scalar.copy(out=cgs[:, 1:T + 1], in_=bcg[:, 1:T + 1]))
                        S.op(DVE, [rbhh, rcgs], [ru], lambda u=u, bhh=bhh, cgs=cgs: nc.vector.tensor_tensor(out=u[:, 1:T + 1], in0=bhh[:, 1:T + 1], in1=cgs[:, 1:T + 1], op=ALU.mult))
                        S.op(DVE, [], [ru], lambda u=u: nc.vector.memset(u[:, 0:1], 0.0))
                        S.op(DVE, [], [ru], lambda u=u: nc.vector.memset(u[:, T + 1:T + 2], 0.0))
                    else:
                        S.op(ACT, [rbcg], [rcgs], lambda cgs=cgs, bcg=bcg: nc.scalar.copy(out=cgs[:, 0:NH], in_=bcg[:, 0:NH]))
                        for (ecol, ei) in ((511, 0), (2560, 1)):
                            if not (t0 - 1 <= ecol <= t0 + T):
                                continue
                            j = ecol - t0 + 1
                            S.op(DVE, [rcgs, R_const], [rcgs], lambda cgs=cgs, j=j, ei=ei: nc.vector.tensor_scalar(
                                out=cgs[:, j:j + 1], in0=cgs[:, j:j + 1], scalar1=emask[:, ei:ei + 1], scalar2=None, op0=ALU.mult))
                        S.op(DVE, [rbhh, rcgs], [ru], lambda u=u, bhh=bhh, cgs=cgs: nc.vector.tensor_tensor(out=u[:, 0:NH], in0=bhh[:, 0:NH], in1=cgs[:, 0:NH], op=ALU.mult))
                    cw = lambda k, ch=ch: gv[:, l, 32 + k * 8 + ch:32 + k * 8 + ch + 1]
                    cb = gv[:, l, 56 + ch:56 + ch + 1]
                    yv = ys[:, c4, :]
                    S.op(DVE, [ru, R_const], [R_ys[c4]], lambda u=u, yv=yv, cw=cw, cb=cb: nc.vector.tensor_scalar(
                        out=yv[:, 0:T], in0=u[:, 1:T + 1], scalar1=cw(1), scalar2=cb, op0=ALU.mult, op1=ALU.add))
                    S.op(DVE, [ru, R_const, R_ys[c4]], [R_ys[c4]], lambda u=u, yv=yv, cw=cw: nc.vector.scalar_tensor_tensor(
                        out=yv[:, 0:T], in0=u[:, 0:T], scalar=cw(0), in1=yv[:, 0:T], op0=ALU.mult, op1=ALU.add))
                    S.op(DVE, [ru, R_const, R_ys[c4]], [R_ys[c4]], lambda u=u, yv=yv, cw=cw: nc.vector.scalar_tensor_tensor(
                        out=yv[:, 0:T], in0=u[:, 2:T + 2], scalar=cw(2), in1=yv[:, 0:T], op0=ALU.mult, op1=ALU.add))
                bgS, rbg = load_slab(l, 5 + half * 3)
                for c4 in range(4):
                    ch = half * 4 + c4
                    bk, rb = bank()
                    mm_group(bk[:, 0:T], rb, bgS, rbg, c4 * 128, lambda kc: hb[:, kc, m0:m0 + T], R_hb, T)
                    co, rco = wf.get()
                    S.op(DVE, [rb, R_ys[c4]], [rco], lambda co=co, bk=bk, c4=c4: nc.vector.tensor_tensor(out=co[:, 0:T], in0=bk[:, 0:T], in1=ys[:, c4, 0:T], op=ALU.mult))
                    sq, rsq = wbp.get()
                    S.op(ACT, [rco], [rsq], lambda sq=sq, co=co: nc.scalar.activation(out=sq[:, 0:T], in_=co[:, 0:T], func=AF.Square))
                    pe_defer.append(lambda sq=sq, ch=ch, rsq=rsq: S.group(PE, [rsq, R_const], [rC], [lambda: nc.tensor.matmul(bkC[:, 0:T], lhsT=ones_b[:], rhs=sq[:, 0:T], start=(ch == 0), stop=(ch == 7))]))
                    flush_defer(2)
                    S.op(ACT, [rco, R_const], [R_cog[ch]], lambda co=co, ch=ch: nc.scalar.activation(out=cog[:, ch, 0:T], in_=co[:, 0:T], func=AF.Identity,
                                                                                                   scale=gv[:, l, 64 + ch:64 + ch + 1]))
            flush_defer(0)
            rstd_from(bkC, rC, rstc[:, 0:T], R_rc, T, 1024)
            for ch in range(8):
                S.op(DVE, [R_cog[ch], R_rc], [R_cog[ch]], lambda ch=ch: nc.vector.tensor_tensor(out=cog[:, ch, 0:T], in0=cog[:, ch, 0:T], in1=rstc[:, 0:T], op=ALU.mult))

            bkS, rSa = ps[:, 7, :], R_bank[7]
            ND = [(ps[:, 4, :], R_bank[4], ps[:, 5, :], R_bank[5]), (ps[:, 6, :], R_bank[6], ps[:, 3, :], R_bank[3])]
            gen_n[0] = 3
            units = [(i, g) for i in range(nb) for g in range(4)]

            def keylist(i):
                ck = [("c", 0, None, None), ("c", 1, None, None)]
                if is_ctx:
                    return ck
                B = B0 + i
                return [("l", B - 1, maskL4, B - 1), ("l", B, None, B), ("l", B + 1, maskR4, B + 1)] + ck

            def emit_scores(ui):
                i, g = units[ui]
                pi = ui % 2
                qc = slice(i * 128, (i + 1) * 128)
                for kbi, (kt, blk, msk, kb) in enumerate(keylist(i)):
                    bk, rb = bank()
                    if kt == "c":
                        kap = lambda par, blk=blk: Kcp[:, g, par, blk * 128:(blk + 1) * 128]
                        rk = R_Kc[blk]
                    else:
                        slot = blk % RK
                        kap = lambda par, slot=slot: Kpad[:, g, par, slot * 128:(slot + 1) * 128]
                        rk = R_K[slot]
                    fns = []
                    if msk is not None:
                        fns.append(lambda bk=bk, msk=msk: nc.tensor.matmul(bk[:, 0:512], lhsT=ident_b[:], rhs=msk[:], start=True, stop=False))
                    for gm in range(4):
                        hq = 4 * g + gm
                        fns.append(lambda bk=bk, gm=gm, hq=hq, kap=kap, msk=msk: nc.tensor.matmul(
                            bk[:, gm * 128:(gm + 1) * 128], lhsT=kap(hq % 2), rhs=qrot[:, hq // 2, qc],
                            start=(msk is None), stop=(gm == 3 or msk is None)))
                    S.group(PE, [rk, R_qrot[2 * g], R_qrot[2 * g + 1], R_const], [rb], fns)
                    if kb is None:
                        S.op(ACT, [rb], [R_PT[pi][kbi]], lambda bk=bk, kbi=kbi: nc.scalar.activation(out=PT[:, pi, kbi, :], in_=bk[:, 0:512], func=AF.Exp, scale=0.125))
                    else:
                        S.op(ACT, [rb, R_const], [R_PT[pi][kbi]], lambda bk=bk, kbi=kbi, kb=kb: nc.scalar.activation(
                            out=PT[:, pi, kbi, :], in_=bk[:, 0:512], func=AF.Exp, scale=0.125, bias=kbias[:, kb:kb + 1]))

            def emit_pv(ui):
                i, g = units[ui]
                pi = ui % 2
                bkN, rN, bkD, rD = ND[ui % 2]
                qc = slice(i * 128, (i + 1) * 128)
                keys = keylist(i)
                nk = len(keys)
                for kbi, (kt, blk, msk, kb) in enumerate(keys):
                    if kt == "c":
                        vap, rv = Vcd[:, blk, g], R_Vc[blk]
                    else:
                        vap, rv = Vd[:, blk % RK, g], R_V[blk % RK]
                    fns = [lambda vap=vap, kbi=kbi: nc.tensor.matmul(bkN[:, 0:512], lhsT=vap.rearrange("p a d -> p (a d)"), rhs=PT[:, pi, kbi, :],
                                                                   start=(kbi == 0), stop=(kbi == nk - 1)),
                           lambda kbi=kbi: nc.tensor.matmul(bkD[:, 0:512], lhsT=ones_b[:], rhs=PT[:, pi, kbi, :],
                                                            start=(kbi == 0), stop=(kbi == nk - 1))]
                    S.group(PE, [rv, R_PT[pi][kbi], R_const], [rN, rD], fns)
                dt, rdt = wf.get()
                for gm in range(4):
                    hq = 4 * g + gm
                    S.op(DVE, [rD, R_const], [rdt], lambda dt=dt, gm=gm, hq=hq: nc.vector.tensor_scalar(
                        out=dt[:, gm * 128:(gm + 1) * 128], in0=bkD[:, gm * 128:(gm + 1) * 128], scalar1=es[:, l * 16 + hq:l * 16 + hq + 1],
                        scalar2=None, op0=ALU.add))
                S.op(DVE, [rdt], [rdt], lambda dt=dt: nc.vector.reciprocal(out=dt[:, 0:512], in_=dt[:, 0:512]))
                o, ro = wf.get()
                sq, rsq = wbp.get()
                for gm in range(4):
                    hq = 4 * g + gm
                    pr = slice((hq % 2) * 64, (hq % 2) * 64 + 64)
                    cc = gm // 2
                    S.op(DVE, [rN, rdt], [ro], lambda o=o, dt=dt, gm=gm, pr=pr, cc=cc: nc.vector.tensor_tensor(
                        out=o[pr, cc * 128:(cc + 1) * 128], in0=bkN[pr, gm * 128:(gm + 1) * 128], in1=dt[pr, gm * 128:(gm + 1) * 128], op=ALU.mult))
                act_defer.append(lambda o=o, sq=sq, ro=ro, rsq=rsq: S.op(ACT, [ro], [rsq], lambda: nc.scalar.activation(
                    out=sq[:, 0:256], in_=o[:, 0:256], func=AF.Square)))
                for cc in range(2):
                    ch = 2 * g + cc
                    act_defer.append(lambda o=o, cc=cc, ch=ch, ro=ro: S.op(ACT, [ro, R_const], [R_aog[ch]], lambda: nc.scalar.activation(
                        out=aog[:, ch, qc], in_=o[:, cc * 128:(cc + 1) * 128], func=AF.Identity, scale=gv[:, l, 72 + ch:72 + ch + 1])))
                fns = [lambda sq=sq, cc=cc: nc.tensor.matmul(bkS[:, qc], lhsT=ones_b[:], rhs=sq[:, cc * 128:(cc + 1) * 128],
                                                             start=(g == 0 and cc == 0), stop=(g == 3 and cc == 1)) for cc in range(2)]
                act_defer.append(lambda fns=fns, rsq=rsq: pe_defer.append(lambda: S.group(PE, [rsq, R_const], [rSa], fns)))

            emit_scores(0)
            for ui in range(len(units)):
                if ui + 1 < len(units):
                    emit_scores(ui + 1)
                flush_act()
                emit_pv(ui)
                flush_defer(1)
            slab0, rs0 = load_slab(l, 9)
            pre = []
            for o4 in range(3):
                bk, rb = bank()
                fns = [(lambda kc=kc, bk=bk, o4=o4: nc.tensor.matmul(bk[:, 0:T], lhsT=slab0[:, kc, o4 * 128:(o4 + 1) * 128], rhs=cog[:, kc, 0:T],
                                                                    start=(kc == 0), stop=False)) for kc in range(8)]
                S.group(PE, [rs0] + R_cog[0:4], [rb], fns[0:4])
                S.group(PE, [rs0] + R_cog[4:8], [rb], fns[4:8])
                pre.append((bk, rb))
            flush_act()
            flush_defer(0)
            gen_n[0] = 4
            rstd_from(bkS, rSa, rsta[:, 0:T], R_ra, T, 1024)
            for ch in range(8):
                S.op(DVE, [R_aog[ch], R_ra], [R_aog[ch]], lambda ch=ch: nc.vector.tensor_tensor(out=aog[:, ch, 0:T], in0=aog[:, ch, 0:T], in1=rsta[:, 0:T], op=ALU.mult))

            bk2s, r2s = ps[:, 6, :], R_bank[6]
            for og in range(4):
                slab, rs = (slab0, rs0) if og == 0 else load_slab(l, 9 + og)
                for o4 in range(4):
                    oc = og * 4 + o4
                    if og == 0 and o4 < 3:
                        bk, rb = pre[o4]
                        fns = [(lambda kc=kc, bk=bk, o4=o4: nc.tensor.matmul(bk[:, 0:T], lhsT=slab0[:, kc, o4 * 128:(o4 + 1) * 128], rhs=aog[:, kc - 8, 0:T],
                                                                            start=False, stop=(kc == KC - 1))) for kc in range(8, KC)]
                        S.group(PE, [rs0] + R_aog[0:4], [rb], fns[0:4])
                        S.group(PE, [rs0] + R_aog[4:8], [rb], fns[4:8])
                    else:
                        bk, rb = bank()
                        mm_group(bk[:, 0:T], rb, slab, rs, o4 * 128,
                                 lambda kc: (cog[:, kc, 0:T] if kc < 8 else aog[:, kc - 8, 0:T]), R_cog + R_aog, T, split=(og == 0))
                    S.op(DVE, [rb, R_xt[oc]] + MR, [R_xt[oc]], lambda bk=bk, oc=oc: nc.vector.scalar_tensor_tensor(
                        out=xt[:, oc, m0:m0 + T], in0=bk[:, 0:T], scalar=G1(oc), in1=xt[:, oc, m0:m0 + T], op0=ALU.mult, op1=ALU.add))
                    sq, rsq = wbp.get()
                    S.op(ACT, [R_xt[oc]], [rsq], lambda sq=sq, oc=oc: nc.scalar.activation(out=sq[:, 0:T], in_=xt[:, oc, m0:m0 + T], func=AF.Square))
                    pe_defer.append(lambda sq=sq, oc=oc, rsq=rsq: S.group(PE, [rsq, R_const], [r2s], [lambda: nc.tensor.matmul(bk2s[:, 0:T], lhsT=ones_b[:], rhs=sq[:, 0:T], start=(oc == 0), stop=(oc == KC - 1))]))
                    flush_defer(2)
            flush_defer(0)
            rstd_from(bk2s, r2s, rstd2[:, 0:T], R_r2, T, D)
            for kc in range(KC):
                t, rt = wf.get()
                S.op(DVE, [R_xt[kc], R_r2], [rt], lambda kc=kc, t=t: nc.vector.tensor_tensor(out=t[:, 0:T], in0=xt[:, kc, m0:m0 + T], in1=rstd2[:, 0:T], op=ALU.mult))
                S.op(ACT, [rt] + MR, [R_hb[kc]], lambda kc=kc, t=t: nc.scalar.activation(out=hb[:, kc, 0:T], in_=t[:, 0:T], func=AF.Identity,
                                                                                          scale=A2(kc), bias=B2(kc)))

            for j in range(4):
                for s in range(4):
                    slab, rs = load_slab(l, 13 + 8 * j + s)
                    for f4 in range(4):
                        fc = s * 4 + f4
                        bk, rb = bank()
                        mm_group(bk[:, 0:T], rb, slab, rs, f4 * 128, lambda kc: hb[:, kc, 0:T], R_hb, T, split=(j == 0 and s == 0 and f4 == 0))
                        r, rr = wf.get()
                        S.op(ACT, [rb], [rr], lambda r=r, bk=bk: nc.scalar.activation(out=r[:, 0:T], in_=bk[:, 0:T], func=AF.Relu))
                        S.op(DVE, [rr], [R_hid[fc]], lambda r=r, fc=fc: nc.vector.tensor_tensor(out=hid[:, fc, 0:T], in0=r[:, 0:T], in1=r[:, 0:T], op=ALU.mult))
                    if s % 2 == 1:
                        adaln_hook()
                for g in range(4):
                    slab, rs = load_slab(l, 13 + 8 * j + 4 + g)
                    for o4 in range(4):
                        oc = g * 4 + o4
                        bk, rb = ps[:, 4 + o4, :], R_bank[4 + o4]
                        fns = [(lambda fc=fc, bk=bk, o4=o4, slab=slab: nc.tensor.matmul(bk[:, 0:T], lhsT=slab[:, fc, o4 * 128:(o4 + 1) * 128], rhs=hid[:, fc, 0:T],
                                                                                       start=(fc == 0), stop=(fc == KC - 1))) for fc in range(KC)]
                        if g == 0 and o4 == 0:
                            for f0 in range(0, KC, 4):
                                S.group(PE, [rs] + R_hid[f0:f0 + 4], [rb], fns[f0:f0 + 4])
                        else:
                            S.group(PE, [rs] + R_hid, [rb], fns)
                        if j == 3 and (is_ctx or not last):
                            t, rt = wf.get()
                            S.op(DVE, [rb, R_xt[oc]] + MR, [rt], lambda bk=bk, oc=oc, t=t: nc.vector.scalar_tensor_tensor(
                                out=t[:, 0:T], in0=bk[:, 0:T], scalar=G2(oc), in1=xt[:, oc, m0:m0 + T], op0=ALU.mult, op1=ALU.add))
                            sti = st_i[0]
                            st_i[0] = (sti + 1) % 4
                            if is_ctx:
                                dstv = fm(cs[l % 2])
                                S.dma(POOL, d_st[sti], [rt], [R_cs[l % 2][oc]], lambda t=t, oc=oc, dstv=dstv: nc.gpsimd.dma_start(out=dstv[:, oc, 0:T], in_=t[:, 0:T]))
                            else:
                                dstv = fm(xs[l % 2])
                                S.dma(POOL, d_st[sti], [rt], [R_xs[l % 2][b][oc] for b in range(B0, B0 + nb)],
                                      lambda t=t, oc=oc, dstv=dstv: nc.gpsimd.dma_start(out=dstv[:, oc, t0:t0 + T], in_=t[:, 0:T]))
                        else:
                            S.op(DVE, [rb, R_xt[oc]] + MR, [R_xt[oc]], lambda bk=bk, oc=oc: nc.vector.scalar_tensor_tensor(
                                out=xt[:, oc, m0:m0 + T], in0=bk[:, 0:T], scalar=G2(oc), in1=xt[:, oc, m0:m0 + T], op0=ALU.mult, op1=ALU.add))

            if is_ctx or not last:
                pass
            else:
                bkF, rF = ps[:, 7, :], R_bank[7]
                for kc in range(KC):
                    sq, rsq = wbp.get()
                    S.op(ACT, [R_xt[kc]], [rsq], lambda sq=sq, kc=kc: nc.scalar.activation(out=sq[:, 0:T], in_=xt[:, kc, m0:m0 + T], func=AF.Square))
                    S.group(PE, [rsq, R_const], [rF], [lambda sq=sq, kc=kc: nc.tensor.matmul(bkF[:, 0:T], lhsT=ones_b[:], rhs=sq[:, 0:T], start=(kc == 0), stop=(kc == KC - 1))])
                rstd_from(bkF, rF, rstd2[:, 0:T], R_r2, T, D)
                yv = fm(y)
                oc0 = (B0 - OWNB) * 128
                for kc in range(KC):
                    t, rt = wf.get()
                    S.op(DVE, [R_xt[kc], R_r2], [rt], lambda kc=kc, t=t: nc.vector.tensor_tensor(out=t[:, 0:T], in0=xt[:, kc, m0:m0 + T], in1=rstd2[:, 0:T], op=ALU.mult))
                    S.op(ACT, [rt, R_const], [rt], lambda kc=kc, t=t: nc.scalar.activation(out=t[:, 0:T], in_=t[:, 0:T], func=AF.Identity, scale=gf[:, kc:kc + 1]))
                    sti = st_i[0]
                    st_i[0] = (sti + 1) % 4
                    S.dma(ACT, d_st[sti], [rt], [], lambda kc=kc, t=t: nc.scalar.dma_start(out=yv[:, kc, oc0:oc0 + T], in_=t[:, 0:T]))

        emit_adaln(0, 0, 48)
        emit_AA(0)
        for l in range(depth):
            tiles = [("ctx", 0, 2, False)] + [("lat", b0, nb, f) for (b0, nb, f) in layer_tiles(l + off)]
            nt = len(tiles)
            perc = -(-NSLAB // (nt - 1))
            if l + 1 < depth:
                ada_pending.extend((l + 1, sl) for sl in range(48))
            for ti, (kind, b0, nb, f) in enumerate(tiles):
                emit_tile(l, kind, b0, nb, f)
                if l + 1 < depth:
                    gate = Res()
                    gate.w = (PE, S.cnt[PE])
                    emit_casts(l + 1, min(NSLAB, ti * perc), min(NSLAB, (ti + 1) * perc), [gate])
            if l + 1 < depth:
                while ada_pending:
                    adaln_hook()
                emit_AA(l + 1)

        for dc in d_st + d_x + d_rope + d_slab + [d_misc]:
            if S.cnt[dc] > 0:
                nc.scalar.wait_ge(S.sem[dc], S.cnt[dc])
        for dc in d_cast:
            if S.cnt[dc] > 0:
                nc.gpsimd.wait_ge(S.sem[dc], S.cnt[dc])
    return nc


def _fm_vec(v):
    v = np.asarray(v, np.float32)
    lead = v.shape[:-1]
    n = v.shape[-1] // 128
    v = v.reshape(lead + (n, 128))
    return np.ascontiguousarray(np.moveaxis(v, -1, 0))


def _rope_tables(a):
    pos = np.arange(a - 512, a - 512 + EXT, dtype=np.int64)
    posc = np.clip(pos, 0, SEQ - 1)
    row = (posc // 64).astype(np.float32)
    col = (posc % 64).astype(np.float32)
    inv = (np.float32(10000.0) ** (-np.arange(0, 32, 2, dtype=np.float32) / np.float32(32))).astype(np.float32)
    C = np.zeros((128, EXT), np.float32)
    Sg = np.zeros((128, EXT), np.float32)
    for p in range(128):
        d = p % 64
        f = d % 16
        axis = row if d < 32 else col
        ang = (axis * inv[f]).astype(np.float32)
        C[p] = np.cos(ang)
        sn = np.sin(ang)
        Sg[p] = -sn if (d % 32) < 16 else sn
    return C, Sg


def _consts():
    cm = np.zeros((128, 4, 128), np.float32)
    idx = np.arange(128)
    cm[idx, 0, idx] = 1.0
    cm[idx, 1, idx ^ 16] = 1.0
    j = idx[:, None]
    r = idx[None, :]
    cm[:, 2, :] = np.where(j >= r, 0.0, NEG)
    cm[:, 3, :] = np.where(j <= r, 0.0, NEG)
    return cm


def make_in_maps(inp, n_cores=8):
    x = np.asarray(inp["x"], np.float32)
    ctx = np.asarray(inp["ctx"], np.float32)
    c = np.asarray(inp["c"], np.float32)
    c_ctx = np.asarray(inp["c_ctx"], np.float32)
    shared = {
        "w_ada": np.ascontiguousarray(np.asarray(inp["w_ada"], np.float32)),
        "w_in": np.ascontiguousarray(np.asarray(inp["w_in"], np.float32)),
        "w_out": np.ascontiguousarray(np.asarray(inp["w_out"], np.float32)),
        "w_mlp1": np.ascontiguousarray(np.asarray(inp["w_mlp1"], np.float32)),
        "w_mlp2": np.ascontiguousarray(np.asarray(inp["w_mlp2"], np.float32)),
        "b_ada": _fm_vec(inp["b_ada"]),
        "gfin": _fm_vec(inp["g_final"]),
        "cmat": _consts(),
    }
    gv = np.zeros((128, DEPTH, 80), np.float32)
    gv[:, :, 0:16] = _fm_vec(inp["g_norm1"])
    gv[:, :, 16:32] = _fm_vec(inp["g_norm2"])
    cw = _fm_vec(inp["conv_w"])
    gv[:, :, 32:56] = cw.reshape(128, DEPTH, 24)
    gv[:, :, 56:64] = _fm_vec(inp["conv_b"])
    gv[:, :, 64:72] = _fm_vec(inp["g_out_conv"])
    gv[:, :, 72:80] = _fm_vec(inp["g_out_attn"])
    shared["gvec"] = gv
    shared["sinkr"] = np.ascontiguousarray(np.broadcast_to(np.asarray(inp["sink"], np.float32).reshape(1, DEPTH * 16), (128, DEPTH * 16)))
    maps = []
    for core in range(n_cores):
        b = core // 4
        a = (core % 4) * 2048
        lo, hi = a - 512, a + 2048 + 512
        xe = np.zeros((EXT, D), np.float32)
        s0, s1 = max(lo, 0), min(hi, SEQ)
        xe[s0 - lo:s1 - lo] = x[b, s0:s1]
        m = dict(shared)
        m["x0"] = np.ascontiguousarray(xe.T)
        m["ctx0"] = np.ascontiguousarray(ctx[b].T)
        cvv = np.zeros((128, KC, 2), np.float32)
        cvv[:, :, 0] = _fm_vec(c[b])
        cvv[:, :, 1] = _fm_vec(c_ctx)
        m["cvec"] = cvv
        C, Sg = _rope_tables(a)
        m["ropeC"] = C
        m["ropeS"] = Sg
        kb = np.zeros((128, NBE), np.float32)
        for blk in range(NBE):
            t = lo + blk * 128
            if t < 0 or t >= SEQ:
                kb[:, blk] = NEG
        m["kbias"] = kb
        em = np.ones((128, 2), np.float32)
        if lo + 511 < 0:
            em[:, 0] = 0.0
        if lo + 2560 >= SEQ:
            em[:, 1] = 0.0
        m["emask"] = em
        maps.append(m)
    return maps


_NC_CACHE = {}


def kernel(**inputs):
    if "nc" not in _NC_CACHE:
        _NC_CACHE["nc"] = build_nc(DEPTH)
    nc = _NC_CACHE["nc"]
    maps = make_in_maps(inputs, 8)
    res = run_bass_kernel_spmd(nc, maps, core_ids=list(range(8)))
    out = np.zeros((2, SEQ, D), np.float32)
    for core in range(8):
        b = core // 4
        a = (core % 4) * 2048
        out[b, a:a + 2048, :] = res.results[core]["y"].T
    return out
```

```python
from contextlib import ExitStack

import numpy as np

import concourse.bass as bass
import concourse.mybir as mybir
from concourse.bass_utils import run_bass_kernel_spmd

F32 = mybir.dt.float32
BF16 = mybir.dt.bfloat16
AF = mybir.ActivationFunctionType
ALU = mybir.AluOpType

PE, ACT, DVE, POOL, SP = 0, 1, 2, 3, 4

D = 2048
KC = 16
DEPTH = 4
SEQ = 8192
NCTX = 256
EXT = 3072
NBE = 24
OWNB = 4
TMAX = 384
WMAX = 514
DFF = 8192
EPS = 1e-6
NEG = -30000.0
RK = 6
NSLAB = 45


class Res:
    __slots__ = ("w", "r")

    def __init__(self):
        self.w = None
        self.r = {}


class Sched:
    def __init__(self, nc, stack):
        self.nc = nc
        self.stack = stack
        self.h = [nc.tensor, nc.scalar, nc.vector, nc.gpsimd, nc.sync]
        self.sem = []
        self.cnt = []
        for n in ["pe", "act", "dve", "pool", "sp"]:
            self.sem.append(stack.enter_context(nc.semaphore("s_" + n)))
            self.cnt.append(0)
        self.known = [dict() for _ in range(5)]

    def new_dma_counter(self, name):
        self.sem.append(self.stack.enter_context(self.nc.semaphore(name)))
        self.cnt.append(0)
        return len(self.sem) - 1

    def _deps(self, reads, writes):
        deps = {}
        for r in reads:
            if r.w is not None:
                c, v = r.w
                if deps.get(c, 0) < v:
                    deps[c] = v
        for w in writes:
            if w.w is not None:
                c, v = w.w
                if deps.get(c, 0) < v:
                    deps[c] = v
            for c, v in w.r.items():
                if deps.get(c, 0) < v:
                    deps[c] = v
        return deps

    def _wait(self, eng, deps):
        kn = self.known[eng]
        for c, v in deps.items():
            if c == eng and eng == PE:
                continue
            if kn.get(c, 0) >= v:
                continue
            self.h[eng].wait_ge(self.sem[c], v)
            kn[c] = v

    def _mark(self, reads, writes, ticket):
        c, v = ticket
        for r in reads:
            r.r[c] = v
        for w in writes:
            w.w = ticket
            w.r = {}

    def op(self, eng, reads, writes, fn):
        self._wait(eng, self._deps(reads, writes))
        ins = fn()
        self.cnt[eng] += 1
        ins.then_inc(self.sem[eng], 1)
        self._mark(reads, writes, (eng, self.cnt[eng]))

    def group(self, eng, reads, writes, fns):
        self._wait(eng, self._deps(reads, writes))
        ins = None
        for fn in fns:
            ins = fn()
        self.cnt[eng] += 1
        ins.then_inc(self.sem[eng], 1)
        self._mark(reads, writes, (eng, self.cnt[eng]))

    def dma(self, q, dc, reads, writes, fn):
        deps = self._deps(reads, writes)
        if self.cnt[dc] > 0:
            deps[dc] = max(deps.get(dc, 0), self.cnt[dc])
        self._wait(q, deps)
        ins = fn()
        self.cnt[dc] += 16
        ins.then_inc(self.sem[dc], 16)
        self._mark(reads, writes, (dc, self.cnt[dc]))


class Pool:
    def __init__(self, aps):
        self.aps = aps
        self.res = [Res() for _ in aps]
        self.i = 0

    def get(self):
        i = self.i
        self.i = (i + 1) % len(self.aps)
        return self.aps[i], self.res[i]


def layer_tiles(l):
    lo, hi = 1 + l, 23 - l
    sizes = [2]
    rem = hi - lo - 2
    while rem > 0:
        nb = min(3, rem)
        sizes.append(nb)
        rem -= nb
    if sizes[-1] == 1:
        sizes[-2:] = [2, 2]
    tiles = []
    b = lo
    for i, nb in enumerate(sizes):
        tiles.append((b, nb, i == 0))
        b += nb
    assert b == hi
    return tiles


def slab_list():
    sl = []
    for c0 in (4096, 3072, 3584, 1024, 2048, 0, 1536, 2560, 512):
        sl.append(("in", 0, c0))
    for og in range(4):
        sl.append(("out", 0, og * 512))
    for j in range(4):
        for s in range(4):
            sl.append(("m1", 0, (4 * j + s) * 512))
        for g in range(4):
            sl.append(("m2", j * 2048, g * 512))
    return sl


def build_nc(depth=DEPTH):
    nc = bass.Bass("TRN2", target_bir_lowering=False)

    def din(name, shape, dt=F32):
        return nc.dram_tensor(name, shape, dt, kind="ExternalInput").ap()

    def dint(name, shape, dt):
        return nc.dram_tensor(name, shape, dt, kind="Internal").ap()

    x0 = din("x0", [D, EXT])
    ctx0 = din("ctx0", [D, NCTX])
    cvec = din("cvec", [128, KC, 2])
    w_ada = din("w_ada", [DEPTH, D, 6 * D])
    b_ada = din("b_ada", [128, DEPTH, 96])
    gvec = din("gvec", [128, DEPTH, 80])
    gfin = din("gfin", [128, KC])
    sinkr = din("sinkr", [128, DEPTH * 16])
    Wd = {"in": din("w_in", [DEPTH, D, 4608]), "out": din("w_out", [DEPTH, D, D]),
          "m1": din("w_mlp1", [DEPTH, D, DFF]), "m2": din("w_mlp2", [DEPTH, DFF, D])}
    ropeC = din("ropeC", [128, EXT])
    ropeS = din("ropeS", [128, EXT])
    kbias_d = din("kbias", [128, NBE])
    emask_d = din("emask", [128, 2])
    cmat = din("cmat", [128, 4, 128])
    y = nc.dram_tensor("y", [D, 2048], F32, kind="ExternalOutput").ap()

    xs = [dint("xsA", [D, EXT], F32), dint("xsB", [D, EXT], F32)]
    cs = [dint("csA", [D, NCTX], F32), dint("csB", [D, NCTX], F32)]
    wb_ada = dint("wb_ada", [DEPTH, D, 6 * D], BF16)
    Wb = {"in": dint("wb_in", [DEPTH, D, 4608], BF16), "out": dint("wb_out", [DEPTH, D, D], BF16),
          "m1": dint("wb_m1", [DEPTH, D, DFF], BF16), "m2": dint("wb_m2", [DEPTH, DFF, D], BF16)}

    def fm(ap2d):
        return ap2d.rearrange("(c p) n -> p c n", p=128)

    with ExitStack() as st:
        S = Sched(nc, st)

        def sb(name, shape, dt):
            return st.enter_context(nc.sbuf_tensor(name, shape, dt))

        xt = sb("xt", [128, KC, WMAX], F32)
        hb = sb("hb", [128, KC, WMAX], BF16)
        slabs = sb("slabs", [128, 3, KC, 512], BF16)
        hid = sb("hid", [128, KC, TMAX], BF16)
        hh2 = sb("hh2", [128, KC, 2], BF16)
        qrot = sb("qrot", [128, 8, TMAX], BF16)
        Kpad = sb("Kpad", [128, 4, 2, RK * 128], BF16)
        Vd = sb("Vd", [128, RK, 4, 2, 64], BF16)
        Kcp = sb("Kcp", [128, 4, 2, NCTX], BF16)
        Vcd = sb("Vcd", [128, 2, 4, 2, 64], BF16)
        PT = sb("PT", [128, 2, 5, 512], BF16)
        ys = sb("ys", [128, 4, TMAX], F32)
        cog = sb("cog", [128, 8, TMAX], BF16)
        aog = sb("aog", [128, 8, TMAX], BF16)
        rstd1 = sb("rstd1", [128, WMAX], F32)
        rstd2 = sb("rstd2", [128, TMAX], F32)
        rstc = sb("rstc", [128, TMAX], F32)
        rsta = sb("rsta", [128, TMAX], F32)
        rpC = sb("rpC", [128, WMAX], F32)
        rpS = sb("rpS", [128, WMAX], F32)
        wf_t = sb("wf", [128, 6, WMAX], F32)
        wb_t = sb("wbp", [128, 4, WMAX], BF16)
        cm_f = sb("cm_f", [128, 4, 128], F32)
        ident_b = sb("ident_b", [128, 128], BF16)
        ones_b = sb("ones_b", [128, 128], BF16)
        maskL4 = sb("maskL4", [128, 512], BF16)
        maskR4 = sb("maskR4", [128, 512], BF16)
        kbias = sb("kbias_s", [128, NBE], F32)
        emask = sb("emask_s", [128, 2], F32)
        es = sb("es", [128, DEPTH * 16], F32)
        mod = sb("mod", [128, DEPTH, 96, 2], F32)
        bada = sb("bada", [128, DEPTH, 96], F32)
        gv = sb("gv", [128, DEPTH, 80], F32)
        gf = sb("gf", [128, KC], F32)
        cv = sb("cv", [128, KC, 2], F32)
        sv = sb("sv", [128, KC, 2], BF16)
        AA = sb("AA", [128, DEPTH, 2, KC, 2], F32)
        ps = st.enter_context(nc.psum_tensor("ps", [128, 8, 512], F32))

        permR = cm_f[:, 1, :]
        R_bank = [Res() for _ in range(8)]
        gen_i = [0]

        gen_n = [4]

        def bank():
            i = gen_i[0] % gen_n[0]
            gen_i[0] = (i + 1) % gen_n[0]
            return ps[:, i, :], R_bank[i]

        wf = Pool([wf_t[:, i, :] for i in range(6)])
        wbp = Pool([wb_t[:, i, :] for i in range(4)])

        R_xt = [Res() for _ in range(KC)]
        R_hb = [Res() for _ in range(KC)]
        R_slab = [Res() for _ in range(3)]
        R_hid = [Res() for _ in range(KC)]
        R_hh2 = Res()
        R_qrot = [Res() for _ in range(8)]
        R_K = [Res() for _ in range(RK)]
        R_V = [Res() for _ in range(RK)]
        R_Kc = [Res(), Res()]
        R_Vc = [Res(), Res()]
        R_PT = [[Res() for _ in range(5)] for _ in range(2)]
        R_ys = [Res() for _ in range(4)]
        R_cog = [Res() for _ in range(8)]
        R_aog = [Res() for _ in range(8)]
        R_r1, R_r2, R_rc, R_ra = Res(), Res(), Res(), Res()
        R_rope = Res()
        R_const = Res()
        R_mod = [Res() for _ in range(DEPTH)]
        R_AA = [Res() for _ in range(DEPTH)]
        R_sv = Res()
        R_xs = [[[Res() for _ in range(KC)] for _ in range(NBE)] for _ in range(2)]
        R_cs = [[Res() for _ in range(KC)] for _ in range(2)]
        R_prog = [[Res() for _ in range(12)] for _ in range(DEPTH)]
        R_wb = [[Res() for _ in range(NSLAB)] for _ in range(DEPTH)]

        d_misc = S.new_dma_counter("d_misc")
        d_x = [S.new_dma_counter("d_x%d" % i) for i in range(4)]
        d_rope = [S.new_dma_counter("d_rp%d" % i) for i in range(2)]
        d_st = [S.new_dma_counter("d_st%d" % i) for i in range(4)]
        d_slab = [S.new_dma_counter("d_sl%d" % i) for i in range(3)]
        d_cast = [S.new_dma_counter("d_c%d" % i) for i in range(NSLAB)]
        d_cada = [S.new_dma_counter("d_a%d" % i) for i in range(24)]
        R_wa = [[Res() for _ in range(24)] for _ in range(DEPTH)]
        st_i = [0]
        slab_i = [0]

        SL = slab_list()

        def emit_casts(l, i0, i1, gate):
            for idx in range(i0, i1):
                nm, r0, c0 = SL[idx]
                src = Wd[nm][l, r0:r0 + 2048, c0:c0 + 512]
                dst = Wb[nm][l, r0:r0 + 2048, c0:c0 + 512]
                S.dma(POOL, d_cast[idx], gate, [R_wb[l][idx]],
                      lambda src=src, dst=dst: nc.gpsimd.dma_start(out=dst, in_=src))

        def emit_cast_ada(l, gate):
            for sl in range(24):
                src = w_ada[l, :, sl * 512:(sl + 1) * 512]
                dst = wb_ada[l, :, sl * 512:(sl + 1) * 512]
                S.dma(POOL, d_cada[sl], gate, [R_wa[l][sl]],
                      lambda src=src, dst=dst: nc.gpsimd.dma_start(out=dst, in_=src))

        emit_cast_ada(0, [])
        emit_casts(0, 0, NSLAB, [])

        def load_slab(l, idx):
            nm, r0, c0 = SL[idx]
            slot = slab_i[0]
            slab_i[0] = (slot + 1) % 3
            src = fm(Wb[nm][l, r0:r0 + 2048, c0:c0 + 512])
            S.dma(SP, d_slab[slot], [R_wb[l][idx]], [R_slab[slot]],
                  lambda: nc.sync.dma_start(out=slabs[:, slot, :, :], in_=src))
            return slabs[:, slot, :, :], R_slab[slot]

        def ld(dst, src, res):
            S.dma(ACT, d_misc, [], [res], lambda: nc.scalar.dma_start(out=dst, in_=src))

        ld(cm_f[:], cmat[:, :, :], R_const)
        ld(kbias[:], kbias_d[:, :], R_const)
        ld(emask[:], emask_d[:, :], R_const)
        ld(es[:], sinkr[:, :], R_const)
        ld(bada[:], b_ada[:, :, :], R_const)
        ld(gv[:], gvec[:, :, :], R_const)
        ld(gf[:], gfin[:, :], R_const)
        ld(cv[:], cvec[:, :, :], R_const)
        C = [R_const]
        S.op(DVE, C, C, lambda: nc.vector.tensor_copy(out=ident_b[:], in_=cm_f[:, 0, :]))
        S.op(DVE, C, C, lambda: nc.vector.memset(ones_b[:], 1.0))
        for g4 in range(4):
            S.op(DVE, C, C, lambda g4=g4: nc.vector.tensor_copy(out=maskL4[:, g4 * 128:(g4 + 1) * 128], in_=cm_f[:, 2, :]))
            S.op(DVE, C, C, lambda g4=g4: nc.vector.tensor_copy(out=maskR4[:, g4 * 128:(g4 + 1) * 128], in_=cm_f[:, 3, :]))
        S.op(DVE, [], R_K, lambda: nc.vector.memset(Kpad[:], 0.0))
        S.op(DVE, [], R_Kc, lambda: nc.vector.memset(Kcp[:], 0.0))
        S.op(ACT, C, C, lambda: nc.scalar.activation(out=es[:], in_=es[:], func=AF.Exp))
        S.op(ACT, C, [R_sv], lambda: nc.scalar.activation(out=sv[:], in_=cv[:], func=AF.Silu))

        def emit_adaln(l, s0, s1):
            for sl in range(s0, s1):
                slot = slab_i[0]
                slab_i[0] = (slot + 1) % 3
                view = slabs[:, slot, :, :]
                src = fm(wb_ada[l, :, sl * 512:(sl + 1) * 512])
                S.dma(SP, d_slab[slot], [R_wa[l][sl]], [R_slab[slot]],
                      lambda view=view, src=src: nc.sync.dma_start(out=view, in_=src))
                for c4 in range(4):
                    cc = sl * 4 + c4
                    bk, rb = bank()
                    fns = [(lambda kc=kc: nc.tensor.matmul(bk[:, 0:2], lhsT=view[:, kc, c4 * 128:(c4 + 1) * 128],
                                                           rhs=sv[:, kc, :], start=(kc == 0), stop=(kc == KC - 1)))
                           for kc in range(KC)]
                    S.group(PE, [R_slab[slot], R_sv], [rb], fns)
                    S.op(ACT, [rb, R_const], [R_mod[l]],
                         lambda cc=cc, bk=bk: nc.scalar.activation(out=mod[:, l, cc, :], in_=bk[:, 0:2], func=AF.Identity,
                                                                  bias=bada[:, l, cc:cc + 1]))

        ada_pending = []

        def adaln_hook():
            if ada_pending:
                la_, sl_ = ada_pending.pop(0)
                emit_adaln(la_, sl_, sl_ + 1)

        def emit_AA(l):
            for ni, (mi, g0) in enumerate(((1, 0), (4, 16))):
                for w in range(2):
                    S.op(DVE, [R_mod[l], R_const], [R_AA[l]],
                         lambda ni=ni, mi=mi, g0=g0, w=w: nc.vector.scalar_tensor_tensor(
                             out=AA[:, l, ni, :, w], in0=mod[:, l, mi * 16:(mi + 1) * 16, w], scalar=1.0,
                             in1=gv[:, l, g0:g0 + 16], op0=ALU.add, op1=ALU.mult))

        def rstd_from(bk, rb, out_ap, rres, n, dim):
            S.op(DVE, [rb], [rres], lambda: nc.vector.tensor_scalar(out=out_ap, in0=bk[:, 0:n], scalar1=1.0 / dim, scalar2=EPS,
                                                                    op0=ALU.mult, op1=ALU.add))
            S.op(ACT, [rres], [rres], lambda: nc.scalar.activation(out=out_ap, in_=out_ap, func=AF.Ln))
            S.op(ACT, [rres], [rres], lambda: nc.scalar.activation(out=out_ap, in_=out_ap, func=AF.Exp, scale=-0.5))

        def mm_group(bk_ap, rb, slab, rs, col0, rhs_fn, rhs_res, n, split=False):
            fns = [(lambda kc=kc: nc.tensor.matmul(bk_ap, lhsT=slab[:, kc, col0:col0 + 128], rhs=rhs_fn(kc),
                                                   start=(kc == 0), stop=(kc == KC - 1))) for kc in range(KC)]
            if split and len(rhs_res) == KC:
                for k0 in range(0, KC, 4):
                    S.group(PE, [rs] + rhs_res[k0:k0 + 4], [rb], fns[k0:k0 + 4])
            else:
                S.group(PE, [rs] + rhs_res, [rb], fns)

        pe_defer = []
        act_defer = []

        def flush_act():
            while act_defer:
                act_defer.pop(0)()

        def flush_defer(keep=0):
            while len(pe_defer) > keep:
                pe_defer.pop(0)()

        off = DEPTH - depth

        def emit_tile(l, kind, B0, nb, first):
            last = (l == depth - 1)
            is_ctx = (kind == "ctx")
            w = 1 if is_ctx else 0
            T = nb * 128
            t0 = B0 * 128
            if is_ctx:
                ts, m0, W, la = 0, 0, NCTX, 0
                kvblocks = [0, 1]
                kv0 = 0
            elif first:
                ts, m0, W, la = t0 - 128, 128, T + 256, 128
                kvblocks = list(range(B0 - 1, B0 + nb + 1))
                kv0 = 0
            else:
                ts, m0, W, la = t0 - 1, 1, T + 129, 128
                kvblocks = list(range(B0 + 1, B0 + nb + 1))
                kv0 = m0 + 128
            kvW = len(kvblocks) * 128
            kv_only = is_ctx and last
            if is_ctx:
                src = ctx0 if l == 0 else cs[(l - 1) % 2]
                src_res = (lambda kc: []) if l == 0 else (lambda kc: [R_cs[(l - 1) % 2][kc]])
            else:
                src = x0 if l == 0 else xs[(l - 1) % 2]
                src_res = (lambda kc: []) if l == 0 else (lambda kc: [R_xs[(l - 1) % 2][b][kc] for b in range(ts // 128, (ts + W - 1) // 128 + 1)])
            srcv = fm(src)
            A1 = lambda kc: AA[:, l, 0, kc, w:w + 1]
            B1 = lambda kc: mod[:, l, kc, w:w + 1]
            G1 = lambda kc: mod[:, l, 32 + kc, w:w + 1]
            B2 = lambda kc: mod[:, l, 48 + kc, w:w + 1]
            A2 = lambda kc: AA[:, l, 1, kc, w:w + 1]
            G2 = lambda kc: mod[:, l, 80 + kc, w:w + 1]
            MR = [R_mod[l], R_AA[l], R_const]

            for q4 in range(4):
                S.dma(ACT, d_x[q4], sum([src_res(kc) for kc in range(q4 * 4, q4 * 4 + 4)], []), R_xt[q4 * 4:(q4 + 1) * 4],
                      lambda q4=q4: nc.scalar.dma_start(out=xt[:, q4 * 4:(q4 + 1) * 4, 0:W], in_=srcv[:, q4 * 4:(q4 + 1) * 4, ts:ts + W]))
            if not is_ctx:
                S.dma(ACT, d_rope[0], [], [R_rope], lambda: nc.scalar.dma_start(out=rpC[:, 0:W], in_=ropeC[:, ts:ts + W]))
                S.dma(ACT, d_rope[1], [], [R_rope], lambda: nc.scalar.dma_start(out=rpS[:, 0:W], in_=ropeS[:, ts:ts + W]))

            WA = W - la
            bkA, rA = ps[:, 6, :], R_bank[6]
            bkB, rB = ps[:, 7, :], R_bank[7]
            for kc in range(KC):
                sq, rsq = wbp.get()
                S.op(ACT, [R_xt[kc]], [rsq], lambda kc=kc, sq=sq: nc.scalar.activation(out=sq[:, 0:W], in_=xt[:, kc, 0:W], func=AF.Square))
                fns = [lambda kc=kc, sq=sq: nc.tensor.matmul(bkA[:, 0:WA], lhsT=ones_b[:], rhs=sq[:, 0:WA], start=(kc == 0), stop=(kc == KC - 1))]
                wr = [rA]
                if la:
                    fns.append(lambda kc=kc, sq=sq: nc.tensor.matmul(bkB[:, 0:la], lhsT=ones_b[:], rhs=sq[:, WA:W], start=(kc == 0), stop=(kc == KC - 1)))
                    wr = [rA, rB]
                S.group(PE, [rsq, R_const], wr, fns)
            rstd_from(bkA, rA, rstd1[:, 0:WA], R_r1, WA, D)
            if la:
                rstd_from(bkB, rB, rstd1[:, WA:W], R_r1, la, D)
            for kc in range(KC):
                t, rt = wf.get()
                S.op(DVE, [R_xt[kc], R_r1], [rt], lambda kc=kc, t=t: nc.vector.tensor_tensor(out=t[:, 0:W], in0=xt[:, kc, 0:W], in1=rstd1[:, 0:W], op=ALU.mult))
                S.op(ACT, [rt] + MR, [R_hb[kc]], lambda kc=kc, t=t: nc.scalar.activation(out=hb[:, kc, 0:W], in_=t[:, 0:W], func=AF.Identity,
                                                                                          scale=A1(kc), bias=B1(kc)))

            slab, rs = load_slab(l, 0)
            for kk in range(2):
                bk, rb = bank()
                mm_group(bk[:, 0:kvW], rb, slab, rs, kk * 128, lambda kc: hb[:, kc, kv0:kv0 + kvW], R_hb, kvW, split=(kk == 0))
                if is_ctx:
                    srcs = lambda rows, c: bk[rows, c:c + 128]
                    rsrc = [rb]
                    for bi, blk in enumerate(kvblocks):
                        for half in range(2):
                            g = 2 * kk + half
                            sr = slice(half * 64, half * 64 + 64)
                            dr = slice((1 - half) * 64, (1 - half) * 64 + 64)
                            S.op(DVE, rsrc, [R_Kc[blk]], lambda g=g, half=half, sr=sr, bi=bi, blk=blk: nc.vector.tensor_copy(
                                out=Kcp[sr, g, half, blk * 128:(blk + 1) * 128], in_=bk[sr, bi * 128:(bi + 1) * 128]))
                            S.op(DVE, rsrc, [R_Kc[blk]], lambda g=g, half=half, sr=sr, dr=dr, bi=bi, blk=blk: nc.vector.tensor_copy(
                                out=Kcp[dr, g, 1 - half, blk * 128:(blk + 1) * 128], in_=bk[sr, bi * 128:(bi + 1) * 128]))
                else:
                    kf, rkf = wf.get()
                    S.op(ACT, [rb], [rkf], lambda kf=kf, bk=bk: nc.scalar.copy(out=kf[:, 0:kvW], in_=bk[:, 0:kvW]))
                    bk2, rb2 = bank()
                    S.group(PE, [rkf, R_const], [rb2], [lambda kf=kf, bk2=bk2: nc.tensor.matmul(bk2[:, 0:kvW], lhsT=permR, rhs=kf[:, 0:kvW], start=True, stop=True)])
                    t1, rt1 = wf.get()
                    S.op(DVE, [rb2, R_rope], [rt1], lambda t1=t1, bk2=bk2: nc.vector.tensor_tensor(out=t1[:, 0:kvW], in0=bk2[:, 0:kvW], in1=rpS[:, kv0:kv0 + kvW], op=ALU.mult))
                    S.op(DVE, [rkf, R_rope], [rkf], lambda kf=kf: nc.vector.tensor_tensor(out=kf[:, 0:kvW], in0=kf[:, 0:kvW], in1=rpC[:, kv0:kv0 + kvW], op=ALU.mult))
                    for bi, blk in enumerate(kvblocks):
                        slot = blk % RK
                        for half in range(2):
                            g = 2 * kk + half
                            sr = slice(half * 64, half * 64 + 64)
                            dr = slice((1 - half) * 64, (1 - half) * 64 + 64)
                            S.op(DVE, [rt1, rkf], [R_K[slot]], lambda g=g, half=half, sr=sr, bi=bi, slot=slot, t1=t1, kf=kf: nc.vector.tensor_tensor(
                                out=Kpad[sr, g, half, slot * 128:(slot + 1) * 128], in0=t1[sr, bi * 128:(bi + 1) * 128],
                                in1=kf[sr, bi * 128:(bi + 1) * 128], op=ALU.add))
                            S.op(DVE, [rt1, rkf], [R_K[slot]], lambda g=g, half=half, sr=sr, dr=dr, bi=bi, slot=slot, t1=t1, kf=kf: nc.vector.tensor_tensor(
                                out=Kpad[dr, g, 1 - half, slot * 128:(slot + 1) * 128], in0=t1[sr, bi * 128:(bi + 1) * 128],
                                in1=kf[sr, bi * 128:(bi + 1) * 128], op=ALU.add))
            for bi, blk in enumerate(kvblocks):
                c = kv0 + bi * 128
                bk, rb = bank()
                fns = [(lambda kc=kc, c=c, bk=bk: nc.tensor.matmul(bk[:, 0:256], lhsT=hb[:, kc, c:c + 128], rhs=slab[:, kc, 256:512],
                                                                  start=(kc == 0), stop=(kc == KC - 1))) for kc in range(KC)]
                S.group(PE, [rs] + R_hb, [rb], fns)
                if is_ctx:
                    vdst, rv = Vcd[:, blk], R_Vc[blk]
                else:
                    vdst, rv = Vd[:, blk % RK], R_V[blk % RK]
                bsrc = bk[:, 0:256].rearrange("p (g d) -> p g d", g=4)
                S.op(ACT, [rb], [rv], lambda vdst=vdst, bsrc=bsrc: nc.scalar.copy(out=vdst[:, :, 0, :], in_=bsrc))
                S.op(DVE, [rb], [rv], lambda vdst=vdst, bsrc=bsrc: nc.vector.tensor_copy(out=vdst[:, :, 1, :], in_=bsrc))
            if kv_only:
                return

            for qs in range(2):
                slab, rs = load_slab(l, 1 + qs)
                for c4 in range(4):
                    ch = qs * 4 + c4
                    bk, rb = bank()
                    mm_group(bk[:, 0:T], rb, slab, rs, c4 * 128, lambda kc: hb[:, kc, m0:m0 + T], R_hb, T)
                    if is_ctx:
                        S.op(ACT, [rb], [R_qrot[ch]], lambda ch=ch, bk=bk: nc.scalar.copy(out=qrot[:, ch, 0:T], in_=bk[:, 0:T]))
                    else:
                        qf, rqf = wf.get()
                        S.op(ACT, [rb], [rqf], lambda qf=qf, bk=bk: nc.scalar.copy(out=qf[:, 0:T], in_=bk[:, 0:T]))
                        bk2, rb2 = bank()
                        S.group(PE, [rqf, R_const], [rb2], [lambda qf=qf, bk2=bk2: nc.tensor.matmul(bk2[:, 0:T], lhsT=permR, rhs=qf[:, 0:T], start=True, stop=True)])
                        t1, rt1 = wf.get()
                        S.op(DVE, [rb2, R_rope], [rt1], lambda t1=t1, bk2=bk2: nc.vector.tensor_tensor(out=t1[:, 0:T], in0=bk2[:, 0:T], in1=rpS[:, m0:m0 + T], op=ALU.mult))
                        S.op(DVE, [rqf, R_rope], [rqf], lambda qf=qf: nc.vector.tensor_tensor(out=qf[:, 0:T], in0=qf[:, 0:T], in1=rpC[:, m0:m0 + T], op=ALU.mult))
                        S.op(DVE, [rt1, rqf], [R_qrot[ch]], lambda ch=ch, t1=t1, qf=qf: nc.vector.tensor_tensor(out=qrot[:, ch, 0:T], in0=t1[:, 0:T], in1=qf[:, 0:T], op=ALU.add))

            bkC, rC = ps[:, 6, :], R_bank[6]
            NH = T + 2
            for half in range(2):
                cgS, rcg = load_slab(l, 3 + half * 3)
                hhS, rhh = load_slab(l, 4 + half * 3)
                for c4 in range(4):
                    ch = half * 4 + c4
                    u_list = []
                    for (sl_, rsl) in ((cgS, rcg), (hhS, rhh)):
                        bk, rb = bank()
                        if is_ctx:
                            mm_group(bk[:, 1:T + 1], rb, sl_, rsl, c4 * 128, lambda kc: hb[:, kc, 0:T], R_hb, T)
                        else:
                            mm_group(bk[:, 0:NH], rb, sl_, rsl, c4 * 128, lambda kc: hb[:, kc, m0 - 1:m0 + T + 1], R_hb, NH)
                        u_list.append((bk, rb))
                    (bcg, rbcg), (bhh, rbhh) = u_list
                    cgs, rcgs = wf.get()
                    u, ru = wf.get()
                    if is_ctx:
                        S.op(ACT, [rbcg], [rcgs], lambda cgs=cgs, bcg=bcg: nc.scalar.copy(out=cgs[:, 1:T + 1], in_=bcg[:, 1:T + 1]))
                        S.op(DVE, [rbhh, rcgs], [ru], lambda u=u, bhh=bhh, cgs=cgs: nc.vector.tensor_tensor(out=u[:, 1:T + 1], in0=bhh[:, 1:T + 1], in1=cgs[:, 1:T + 1], op=ALU.mult))
                        S.op(DVE, [], [ru], lambda u=u: nc.vector.memset(u[:, 0:1], 0.0))
                        S.op(DVE, [], [ru], lambda u=u: nc.vector.memset(u[:, T + 1:T + 2], 0.0))
                    else:
                        S.op(ACT, [rbcg], [rcgs], lambda cgs=cgs, bcg=bcg: nc.scalar.copy(out=cgs[:, 0:NH], in_=bcg[:, 0:NH]))
                        for (ecol, ei) in ((511, 0), (2560, 1)):
                            if not (t0 - 1 <= ecol <= t0 + T):
                                continue
                            j = ecol - t0 + 1
                            S.op(DVE, [rcgs, R_const], [rcgs], lambda cgs=cgs, j=j, ei=ei: nc.vector.tensor_scalar(
                                out=cgs[:, j:j + 1], in0=cgs[:, j:j + 1], scalar1=emask[:, ei:ei + 1], scalar2=None, op0=ALU.mult))
                        S.op(DVE, [rbhh, rcgs], [ru], lambda u=u, bhh=bhh, cgs=cgs: nc.vector.tensor_tensor(out=u[:, 0:NH], in0=bhh[:, 0:NH], in1=cgs[:, 0:NH], op=ALU.mult))
                    cw = lambda k, ch=ch: gv[:, l, 32 + k * 8 + ch:32 + k * 8 + ch + 1]
                    cb = gv[:, l, 56 + ch:56 + ch + 1]
                    yv = ys[:, c4, :]
                    S.op(DVE, [ru, R_const], [R_ys[c4]], lambda u=u, yv=yv, cw=cw, cb=cb: nc.vector.tensor_scalar(
                        out=yv[:, 0:T], in0=u[:, 1:T + 1], scalar1=cw(1), scalar2=cb, op0=ALU.mult, op1=ALU.add))
                    S.op(DVE, [ru, R_const, R_ys[c4]], [R_ys[c4]], lambda u=u, yv=yv, cw=cw: nc.vector.scalar_tensor_tensor(
                        out=yv[:, 0:T], in0=u[:, 0:T], scalar=cw(0), in1=yv[:, 0:T], op0=ALU.mult, op1=ALU.add))
                    S.op(DVE, [ru, R_const, R_ys[c4]], [R_ys[c4]], lambda u=u, yv=yv, cw=cw: nc.vector.scalar_tensor_tensor(
                        out=yv[:, 0:T], in0=u[:, 2:T + 2], scalar=cw(2), in1=yv[:, 0:T], op0=ALU.mult, op1=ALU.add))
                bgS, rbg = load_slab(l, 5 + half * 3)
                for c4 in range(4):
                    ch = half * 4 + c4
                    bk, rb = bank()
                    mm_group(bk[:, 0:T], rb, bgS, rbg, c4 * 128, lambda kc: hb[:, kc, m0:m0 + T], R_hb, T)
                    co, rco = wf.get()
                    S.op(DVE, [rb, R_ys[c4]], [rco], lambda co=co, bk=bk, c4=c4: nc.vector.tensor_tensor(out=co[:, 0:T], in0=bk[:, 0:T], in1=ys[:, c4, 0:T], op=ALU.mult))
                    sq, rsq = wbp.get()
                    S.op(ACT, [rco], [rsq], lambda sq=sq, co=co: nc.scalar.activation(out=sq[:, 0:T], in_=co[:, 0:T], func=AF.Square))
                    pe_defer.append(lambda sq=sq, ch=ch, rsq=rsq: S.group(PE, [rsq, R_const], [rC], [lambda: nc.tensor.matmul(bkC[:, 0:T], lhsT=ones_b[:], rhs=sq[:, 0:T], start=(ch == 0), stop=(ch == 7))]))
                    flush_defer(2)
                    S.op(ACT, [rco, R_const], [R_cog[ch]], lambda co=co, ch=ch: nc.scalar.activation(out=cog[:, ch, 0:T], in_=co[:, 0:T], func=AF.Identity,
                                                                                                   scale=gv[:, l, 64 + ch:64 + ch + 1]))
            flush_defer(0)
            rstd_from(bkC, rC, rstc[:, 0:T], R_rc, T, 1024)
            for ch in range(8):
                S.op(DVE, [R_cog[ch], R_rc], [R_cog[ch]], lambda ch=ch: nc.vector.tensor_tensor(out=cog[:, ch, 0:T], in0=cog[:, ch, 0:T], in1=rstc[:, 0:T], op=ALU.mult))

            bkS, rSa = ps[:, 7, :], R_bank[7]
            ND = [(ps[:, 4, :], R_bank[4], ps[:, 5, :], R_bank[5]), (ps[:, 6, :], R_bank[6], ps[:, 3, :], R_bank[3])]
            gen_n[0] = 3
            units = [(i, g) for i in range(nb) for g in range(4)]

            def keylist(i):
                ck = [("c", 0, None, None), ("c", 1, None, None)]
                if is_ctx:
                    return ck
                B = B0 + i
                return [("l", B - 1, maskL4, B - 1), ("l", B, None, B), ("l", B + 1, maskR4, B + 1)] + ck

            def emit_scores(ui):
                i, g = units[ui]
                pi = ui % 2
                qc = slice(i * 128, (i + 1) * 128)
                for kbi, (kt, blk, msk, kb) in enumerate(keylist(i)):
                    bk, rb = bank()
                    if kt == "c":
                        kap = lambda par, blk=blk: Kcp[:, g, par, blk * 128:(blk + 1) * 128]
                        rk = R_Kc[blk]
                    else:
                        slot = blk % RK
                        kap = lambda par, slot=slot: Kpad[:, g, par, slot * 128:(slot + 1) * 128]
                        rk = R_K[slot]
                    fns = []
                    if msk is not None:
                        fns.append(lambda bk=bk, msk=msk: nc.tensor.matmul(bk[:, 0:512], lhsT=ident_b[:], rhs=msk[:], start=True, stop=False))
                    for gm in range(4):
                        hq = 4 * g + gm
                        fns.append(lambda bk=bk, gm=gm, hq=hq, kap=kap, msk=msk: nc.tensor.matmul(
                            bk[:, gm * 128:(gm + 1) * 128], lhsT=kap(hq % 2), rhs=qrot[:, hq // 2, qc],
                            start=(msk is None), stop=(gm == 3 or msk is None)))
                    S.group(PE, [rk, R_qrot[2 * g], R_qrot[2 * g + 1], R_const], [rb], fns)
                    if kb is None:
                        S.op(ACT, [rb], [R_PT[pi][kbi]], lambda bk=bk, kbi=kbi: nc.scalar.activation(out=PT[:, pi, kbi, :], in_=bk[:, 0:512], func=AF.Exp, scale=0.125))
                    else:
                        S.op(ACT, [rb, R_const], [R_PT[pi][kbi]], lambda bk=bk, kbi=kbi, kb=kb: nc.scalar.activation(
                            out=PT[:, pi, kbi, :], in_=bk[:, 0:512], func=AF.Exp, scale=0.125, bias=kbias[:, kb:kb + 1]))

            def emit_pv(ui):
                i, g = units[ui]
                pi = ui % 2
                bkN, rN, bkD, rD = ND[ui % 2]
                qc = slice(i * 128, (i + 1) * 128)
                keys = keylist(i)
                nk = len(keys)
                for kbi, (kt, blk, msk, kb) in enumerate(keys):
                    if kt == "c":
                        vap, rv = Vcd[:, blk, g], R_Vc[blk]
                    else:
                        vap, rv = Vd[:, blk % RK, g], R_V[blk % RK]
                    fns = [lambda vap=vap, kbi=kbi: nc.tensor.matmul(bkN[:, 0:512], lhsT=vap.rearrange("p a d -> p (a d)"), rhs=PT[:, pi, kbi, :],
                                                                   start=(kbi == 0), stop=(kbi == nk - 1)),
                           lambda kbi=kbi: nc.tensor.matmul(bkD[:, 0:512], lhsT=ones_b[:], rhs=PT[:, pi, kbi, :],
                                                            start=(kbi == 0), stop=(kbi == nk - 1))]
                    S.group(PE, [rv, R_PT[pi][kbi], R_const], [rN, rD], fns)
                dt, rdt = wf.get()
                for gm in range(4):
                    hq = 4 * g + gm
                    S.op(DVE, [rD, R_const], [rdt], lambda dt=dt, gm=gm, hq=hq: nc.vector.tensor_scalar(
                        out=dt[:, gm * 128:(gm + 1) * 128], in0=bkD[:, gm * 128:(gm + 1) * 128], scalar1=es[:, l * 16 + hq:l * 16 + hq + 1],
                        scalar2=None, op0=ALU.add))
                S.op(DVE, [rdt], [rdt], lambda dt=dt: nc.vector.reciprocal(out=dt[:, 0:512], in_=dt[:, 0:512]))
                o, ro = wf.get()
                sq, rsq = wbp.get()
                for gm in range(4):
                    hq = 4 * g + gm
                    pr = slice((hq % 2) * 64, (hq % 2) * 64 + 64)
                    cc = gm // 2
                    S.op(DVE, [rN, rdt], [ro], lambda o=o, dt=dt, gm=gm, pr=pr, cc=cc: nc.vector.tensor_tensor(
                        out=o[pr, cc * 128:(cc + 1) * 128], in0=bkN[pr, gm * 128:(gm + 1) * 128], in1=dt[pr, gm * 128:(gm + 1) * 128], op=ALU.mult))
                act_defer.append(lambda o=o, sq=sq, ro=ro, rsq=rsq: S.op(ACT, [ro], [rsq], lambda: nc.scalar.activation(
                    out=sq[:, 0:256], in_=o[:, 0:256], func=AF.Square)))
                for cc in range(2):
                    ch = 2 * g + cc
                    act_defer.append(lambda o=o, cc=cc, ch=ch, ro=ro: S.op(ACT, [ro, R_const], [R_aog[ch]], lambda: nc.scalar.activation(
                        out=aog[:, ch, qc], in_=o[:, cc * 128:(cc + 1) * 128], func=AF.Identity, scale=gv[:, l, 72 + ch:72 + ch + 1])))
                fns = [lambda sq=sq, cc=cc: nc.tensor.matmul(bkS[:, qc], lhsT=ones_b[:], rhs=sq[:, cc * 128:(cc + 1) * 128],
                                                             start=(g == 0 and cc == 0), stop=(g == 3 and cc == 1)) for cc in range(2)]
                act_defer.append(lambda fns=fns, rsq=rsq: pe_defer.append(lambda: S.group(PE, [rsq, R_const], [rSa], fns)))

            emit_scores(0)
            for ui in range(len(units)):
                if ui + 1 < len(units):
                    emit_scores(ui + 1)
                flush_act()
                emit_pv(ui)
                flush_defer(1)
            slab0, rs0 = load_slab(l, 9)
            pre = []
            for o4 in range(3):
                bk, rb = bank()
                fns = [(lambda kc=kc, bk=bk, o4=o4: nc.tensor.matmul(bk[:, 0:T], lhsT=slab0[:, kc, o4 * 128:(o4 + 1) * 128], rhs=cog[:, kc, 0:T],
                                                                    start=(kc == 0), stop=False)) for kc in range(8)]
                S.group(PE, [rs0] + R_cog[0:4], [rb], fns[0:4])
                S.group(PE, [rs0] + R_cog[4:8], [rb], fns[4:8])
                pre.append((bk, rb))
            flush_act()
            flush_defer(0)
            gen_n[0] = 4
            rstd_from(bkS, rSa, rsta[:, 0:T], R_ra, T, 1024)
            for ch in range(8):
                S.op(DVE, [R_aog[ch], R_ra], [R_aog[ch]], lambda ch=ch: nc.vector.tensor_tensor(out=aog[:, ch, 0:T], in0=aog[:, ch, 0:T], in1=rsta[:, 0:T], op=ALU.mult))

            bk2s, r2s = ps[:, 6, :], R_bank[6]
            for og in range(4):
                slab, rs = (slab0, rs0) if og == 0 else load_slab(l, 9 + og)
                for o4 in range(4):
                    oc = og * 4 + o4
                    if og == 0 and o4 < 3:
                        bk, rb = pre[o4]
                        fns = [(lambda kc=kc, bk=bk, o4=o4: nc.tensor.matmul(bk[:, 0:T], lhsT=slab0[:, kc, o4 * 128:(o4 + 1) * 128], rhs=aog[:, kc - 8, 0:T],
                                                                            start=False, stop=(kc == KC - 1))) for kc in range(8, KC)]
                        S.group(PE, [rs0] + R_aog[0:4], [rb], fns[0:4])
                        S.group(PE, [rs0] + R_aog[4:8], [rb], fns[4:8])
                    else:
                        bk, rb = bank()
                        mm_group(bk[:, 0:T], rb, slab, rs, o4 * 128,
                                 lambda kc: (cog[:, kc, 0:T] if kc < 8 else aog[:, kc - 8, 0:T]), R_cog + R_aog, T, split=(og == 0))
                    S.op(DVE, [rb, R_xt[oc]] + MR, [R_xt[oc]], lambda bk=bk, oc=oc: nc.vector.scalar_tensor_tensor(
                        out=xt[:, oc, m0:m0 + T], in0=bk[:, 0:T], scalar=G1(oc), in1=xt[:, oc, m0:m0 + T], op0=ALU.mult, op1=ALU.add))
                    sq, rsq = wbp.get()
                    S.op(ACT, [R_xt[oc]], [rsq], lambda sq=sq, oc=oc: nc.scalar.activation(out=sq[:, 0:T], in_=xt[:, oc, m0:m0 + T], func=AF.Square))
                    pe_defer.append(lambda sq=sq, oc=oc, rsq=rsq: S.group(PE, [rsq, R_const], [r2s], [lambda: nc.tensor.matmul(bk2s[:, 0:T], lhsT=ones_b[:], rhs=sq[:, 0:T], start=(oc == 0), stop=(oc == KC - 1))]))
                    flush_defer(2)
            flush_defer(0)
            rstd_from(bk2s, r2s, rstd2[:, 0:T], R_r2, T, D)
            for kc in range(KC):
                t, rt = wf.get()
                S.op(DVE, [R_xt[kc], R_r2], [rt], lambda kc=kc, t=t: nc.vector.tensor_tensor(out=t[:, 0:T], in0=xt[:, kc, m0:m0 + T], in1=rstd2[:, 0:T], op=ALU.mult))
                S.op(ACT, [rt] + MR, [R_hb[kc]], lambda kc=kc, t=t: nc.scalar.activation(out=hb[:, kc, 0:T], in_=t[:, 0:T], func=AF.Identity,
                                                                                          scale=A2(kc), bias=B2(kc)))

            for j in range(4):
                for s in range(4):
                    slab, rs = load_slab(l, 13 + 8 * j + s)
                    for f4 in range(4):
                        fc = s * 4 + f4
                        bk, rb = bank()
                        mm_group(bk[:, 0:T], rb, slab, rs, f4 * 128, lambda kc: hb[:, kc, 0:T], R_hb, T, split=(j == 0 and s == 0 and f4 == 0))
                        r, rr = wf.get()
                        S.op(ACT, [rb], [rr], lambda r=r, bk=bk: nc.scalar.activation(out=r[:, 0:T], in_=bk[:, 0:T], func=AF.Relu))
                        S.op(DVE, [rr], [R_hid[fc]], lambda r=r, fc=fc: nc.vector.tensor_tensor(out=hid[:, fc, 0:T], in0=r[:, 0:T], in1=r[:, 0:T], op=ALU.mult))
                    if s % 2 == 1:
                        adaln_hook()
                for g in range(4):
                    slab, rs = load_slab(l, 13 + 8 * j + 4 + g)
                    for o4 in range(4):
                        oc = g * 4 + o4
                        bk, rb = ps[:, 4 + o4, :], R_bank[4 + o4]
                        fns = [(lambda fc=fc, bk=bk, o4=o4, slab=slab: nc.tensor.matmul(bk[:, 0:T], lhsT=slab[:, fc, o4 * 128:(o4 + 1) * 128], rhs=hid[:, fc, 0:T],
                                                                                       start=(fc == 0), stop=(fc == KC - 1))) for fc in range(KC)]
                        if g == 0 and o4 == 0:
                            for f0 in range(0, KC, 4):
                                S.group(PE, [rs] + R_hid[f0:f0 + 4], [rb], fns[f0:f0 + 4])
                        else:
                            S.group(PE, [rs] + R_hid, [rb], fns)
                        if j == 3 and (is_ctx or not last):
                            t, rt = wf.get()
                            S.op(DVE, [rb, R_xt[oc]] + MR, [rt], lambda bk=bk, oc=oc, t=t: nc.vector.scalar_tensor_tensor(
                                out=t[:, 0:T], in0=bk[:, 0:T], scalar=G2(oc), in1=xt[:, oc, m0:m0 + T], op0=ALU.mult, op1=ALU.add))
                            sti = st_i[0]
                            st_i[0] = (sti + 1) % 4
                            if is_ctx:
                                dstv = fm(cs[l % 2])
                                S.dma(POOL, d_st[sti], [rt], [R_cs[l % 2][oc]], lambda t=t, oc=oc, dstv=dstv: nc.gpsimd.dma_start(out=dstv[:, oc, 0:T], in_=t[:, 0:T]))
                            else:
                                dstv = fm(xs[l % 2])
                                S.dma(POOL, d_st[sti], [rt], [R_xs[l % 2][b][oc] for b in range(B0, B0 + nb)],
                                      lambda t=t, oc=oc, dstv=dstv: nc.gpsimd.dma_start(out=dstv[:, oc, t0:t0 + T], in_=t[:, 0:T]))
                        else:
                            S.op(DVE, [rb, R_xt[oc]] + MR, [R_xt[oc]], lambda bk=bk, oc=oc: nc.vector.scalar_tensor_tensor(
                                out=xt[:, oc, m0:m0 + T], in0=bk[:, 0:T], scalar=G2(oc), in1=xt[:, oc, m0:m0 + T], op0=ALU.mult, op1=ALU.add))

            if is_ctx or not last:
                pass
            else:
                bkF, rF = ps[:, 7, :], R_bank[7]
                for kc in range(KC):
                    sq, rsq = wbp.get()
                    S.op(ACT, [R_xt[kc]], [rsq], lambda sq=sq, kc=kc: nc.scalar.activation(out=sq[:, 0:T], in_=xt[:, kc, m0:m0 + T], func=AF.Square))
                    S.group(PE, [rsq, R_const], [rF], [lambda sq=sq, kc=kc: nc.tensor.matmul(bkF[:, 0:T], lhsT=ones_b[:], rhs=sq[:, 0:T], start=(kc == 0), stop=(kc == KC - 1))])
                rstd_from(bkF, rF, rstd2[:, 0:T], R_r2, T, D)
                yv = fm(y)
                oc0 = (B0 - OWNB) * 128
                for kc in range(KC):
                    t, rt = wf.get()
                    S.op(DVE, [R_xt[kc], R_r2], [rt], lambda kc=kc, t=t: nc.vector.tensor_tensor(out=t[:, 0:T], in0=xt[:, kc, m0:m0 + T], in1=rstd2[:, 0:T], op=ALU.mult))
                    S.op(ACT, [rt, R_const], [rt], lambda kc=kc, t=t: nc.scalar.activation(out=t[:, 0:T], in_=t[:, 0:T], func=AF.Identity, scale=gf[:, kc:kc + 1]))
                    sti = st_i[0]
                    st_i[0] = (sti + 1) % 4
                    S.dma(ACT, d_st[sti], [rt], [], lambda kc=kc, t=t: nc.scalar.dma_start(out=yv[:, kc, oc0:oc0 + T], in_=t[:, 0:T]))

        emit_adaln(0, 0, 24)
        emit_AA(0)
        for l in range(depth):
            tiles = [("ctx", 0, 2, False)] + [("lat", b0, nb, f) for (b0, nb, f) in layer_tiles(l + off)]
            nt = len(tiles)
            perc = -(-NSLAB // (nt - 1))
            if l + 1 < depth:
                gate0 = Res()
                if S.cnt[PE] > 0:
                    gate0.w = (PE, S.cnt[PE])
                emit_cast_ada(l + 1, [gate0])
                ada_pending.extend((l + 1, sl) for sl in range(24))
            for ti, (kind, b0, nb, f) in enumerate(tiles):
                emit_tile(l, kind, b0, nb, f)
                if l + 1 < depth:
                    gate = Res()
                    gate.w = (PE, S.cnt[PE])
                    emit_casts(l + 1, min(NSLAB, ti * perc), min(NSLAB, (ti + 1) * perc), [gate])
            if l + 1 < depth:
                while ada_pending:
                    adaln_hook()
                emit_AA(l + 1)

        for dc in d_st + d_x + d_rope + d_slab + [d_misc]:
            if S.cnt[dc] > 0:
                nc.scalar.wait_ge(S.sem[dc], S.cnt[dc])
        for dc in d_cast + d_cada:
            if S.cnt[dc] > 0:
                nc.gpsimd.wait_ge(S.sem[dc], S.cnt[dc])
    return nc


def _fm_vec(v):
    v = np.asarray(v, np.float32)
    lead = v.shape[:-1]
    n = v.shape[-1] // 128
    v = v.reshape(lead + (n, 128))
    return np.ascontiguousarray(np.moveaxis(v, -1, 0))


def _rope_tables(a):
    pos = np.arange(a - 512, a - 512 + EXT, dtype=np.int64)
    posc = np.clip(pos, 0, SEQ - 1)
    row = (posc // 64).astype(np.float32)
    col = (posc % 64).astype(np.float32)
    inv = (np.float32(10000.0) ** (-np.arange(0, 32, 2, dtype=np.float32) / np.float32(32))).astype(np.float32)
    C = np.zeros((128, EXT), np.float32)
    Sg = np.zeros((128, EXT), np.float32)
    for p in range(128):
        d = p % 64
        f = d % 16
        axis = row if d < 32 else col
        ang = (axis * inv[f]).astype(np.float32)
        C[p] = np.cos(ang)
        sn = np.sin(ang)
        Sg[p] = -sn if (d % 32) < 16 else sn
    return C, Sg


def _consts():
    cm = np.zeros((128, 4, 128), np.float32)
    idx = np.arange(128)
    cm[idx, 0, idx] = 1.0
    cm[idx, 1, idx ^ 16] = 1.0
    j = idx[:, None]
    r = idx[None, :]
    cm[:, 2, :] = np.where(j >= r, 0.0, NEG)
    cm[:, 3, :] = np.where(j <= r, 0.0, NEG)
    return cm


def make_in_maps(inp, n_cores=8):
    x = np.asarray(inp["x"], np.float32)
    ctx = np.asarray(inp["ctx"], np.float32)
    c = np.asarray(inp["c"], np.float32)
    c_ctx = np.asarray(inp["c_ctx"], np.float32)
    shared = {
        "w_ada": np.ascontiguousarray(np.asarray(inp["w_ada"], np.float32)),
        "w_in": np.ascontiguousarray(np.asarray(inp["w_in"], np.float32)),
        "w_out": np.ascontiguousarray(np.asarray(inp["w_out"], np.float32)),
        "w_mlp1": np.ascontiguousarray(np.asarray(inp["w_mlp1"], np.float32)),
        "w_mlp2": np.ascontiguousarray(np.asarray(inp["w_mlp2"], np.float32)),
        "b_ada": _fm_vec(inp["b_ada"]),
        "gfin": _fm_vec(inp["g_final"]),
        "cmat": _consts(),
    }
    gv = np.zeros((128, DEPTH, 80), np.float32)
    gv[:, :, 0:16] = _fm_vec(inp["g_norm1"])
    gv[:, :, 16:32] = _fm_vec(inp["g_norm2"])
    cw = _fm_vec(inp["conv_w"])
    gv[:, :, 32:56] = cw.reshape(128, DEPTH, 24)
    gv[:, :, 56:64] = _fm_vec(inp["conv_b"])
    gv[:, :, 64:72] = _fm_vec(inp["g_out_conv"])
    gv[:, :, 72:80] = _fm_vec(inp["g_out_attn"])
    shared["gvec"] = gv
    shared["sinkr"] = np.ascontiguousarray(np.broadcast_to(np.asarray(inp["sink"], np.float32).reshape(1, DEPTH * 16), (128, DEPTH * 16)))
    maps = []
    for core in range(n_cores):
        b = core // 4
        a = (core % 4) * 2048
        lo, hi = a - 512, a + 2048 + 512
        xe = np.zeros((EXT, D), np.float32)
        s0, s1 = max(lo, 0), min(hi, SEQ)
        xe[s0 - lo:s1 - lo] = x[b, s0:s1]
        m = dict(shared)
        m["x0"] = np.ascontiguousarray(xe.T)
        m["ctx0"] = np.ascontiguousarray(ctx[b].T)
        cvv = np.zeros((128, KC, 2), np.float32)
        cvv[:, :, 0] = _fm_vec(c[b])
        cvv[:, :, 1] = _fm_vec(c_ctx)
        m["cvec"] = cvv
        C, Sg = _rope_tables(a)
        m["ropeC"] = C
        m["ropeS"] = Sg
        kb = np.zeros((128, NBE), np.float32)
        for blk in range(NBE):
            t = lo + blk * 128
            if t < 0 or t >= SEQ:
                kb[:, blk] = NEG
        m["kbias"] = kb
        em = np.ones((128, 2), np.float32)
        if lo + 511 < 0:
            em[:, 0] = 0.0
        if lo + 2560 >= SEQ:
            em[:, 1] = 0.0
        m["emask"] = em
        maps.append(m)
    return maps


_NC_CACHE = {}


def kernel(**inputs):
    if "nc" not in _NC_CACHE:
        _NC_CACHE["nc"] = build_nc(DEPTH)
    nc = _NC_CACHE["nc"]
    maps = make_in_maps(inputs, 8)
    res = run_bass_kernel_spmd(nc, maps, core_ids=list(range(8)))
    out = np.zeros((2, SEQ, D), np.float32)
    for core in range(8):
        b = core // 4
        a = (core % 4) * 2048
        out[b, a:a + 2048, :] = res.results[core]["y"].T
    return out
```
